# Optimizing a Trainium2 kernel written in Bass

```python
import jax, jax.numpy as jnp
from jax import lax
import numpy as np

D_MODEL = 1024
BATCH = 2
SEQ = 8192
DEPTH = 2

HEAD_DIM = 64
POOL_WINDOWS = (2, 4, 8, 16)
POOL_GROUPS = 4
POOL_GROUP_DIM = D_MODEL // 16
POOL_WIDTH = POOL_GROUPS * POOL_GROUP_DIM
N_Q_HEADS = D_MODEL // 128
N_KV_HEADS = 2
Q_PER_KV = N_Q_HEADS // N_KV_HEADS
WINDOW = 128
ATTN_BLOCK = 128
ATTN_WIDTH = N_Q_HEADS * HEAD_DIM
KV_WIDTH = N_KV_HEADS * HEAD_DIM
CHUNK = 128
SGU_GROUPS = 4
SGU_GROUP_DIM = D_MODEL // 16
SGU_WIDTH = SGU_GROUPS * SGU_GROUP_DIM
N_BRANCHES = 3
IN_COLS = POOL_WIDTH + ATTN_WIDTH + 2 * KV_WIDTH + 2 * SGU_WIDTH + N_BRANCHES * D_MODEL
D_FF = 2816
CONV_WIDTH = 3
ROPE_THETA = 10000.0
EPS = 1e-6

kernel_name = "hybrid_pool_swa_sgu_convffn"


def rms_norm(x, g):
    xf = x.astype(jnp.float32)
    y = xf * lax.rsqrt(jnp.mean(xf * xf, axis=-1, keepdims=True) + EPS)
    return (y * g.astype(jnp.float32)).astype(x.dtype)


def rope_tables(positions):
    inv_freq = ROPE_THETA ** (-jnp.arange(0, HEAD_DIM, 2, dtype=jnp.float32) / HEAD_DIM)
    ang = positions.astype(jnp.float32)[..., None] * inv_freq
    return jnp.cos(ang)[:, :, None, :], jnp.sin(ang)[:, :, None, :]


def apply_rope(t, cos, sin):
    tf = t.astype(jnp.float32)
    t1, t2 = jnp.split(tf, 2, axis=-1)
    return jnp.concatenate([t1 * cos - t2 * sin, t2 * cos + t1 * sin], axis=-1).astype(t.dtype)


def pool_mixer(xa, w_pool, pool_scale):
    B, S, _ = xa.shape
    xf = xa.astype(jnp.float32)
    cs = jnp.concatenate([jnp.zeros((B, 1, POOL_WIDTH), jnp.float32), jnp.cumsum(xf, axis=1)], axis=1)
    t = jnp.arange(S)
    pooled = []
    for g, w in enumerate(POOL_WINDOWS):
        c = cs[..., g * POOL_GROUP_DIM:(g + 1) * POOL_GROUP_DIM]
        upper = c[:, 1:]
        lower = jnp.concatenate([jnp.zeros((B, w - 1, POOL_GROUP_DIM), jnp.float32), c[:, :S - w + 1]], axis=1)
        count = jnp.minimum(t + 1, w).astype(jnp.float32)[None, :, None]
        pooled.append((upper - lower) / count)
    pooled = jnp.stack(pooled, axis=2)
    diff = (pooled - xf.reshape(B, S, POOL_GROUPS, POOL_GROUP_DIM)).astype(xa.dtype)
    mixed = jnp.einsum('bsgc,gcd->bsgd', diff, w_pool).reshape(B, S, POOL_WIDTH)
    return mixed * pool_scale


def swa_attention(q, k, v, sinks):
    B, S = q.shape[:2]
    nb = S // ATTN_BLOCK
    qb = q.reshape(B, nb, ATTN_BLOCK, N_KV_HEADS, Q_PER_KV, HEAD_DIM)

    def band(t):
        tb = t.reshape(B, nb, ATTN_BLOCK, N_KV_HEADS, HEAD_DIM)
        prev = jnp.concatenate([jnp.zeros_like(tb[:, :1]), tb[:, :-1]], axis=1)
        return jnp.concatenate([prev, tb], axis=2)

    kb, vb = band(k), band(v)
    scores = jnp.einsum('bnqhgd,bnkhd->bnhgqk', qb, kb).astype(jnp.float32) * (HEAD_DIM ** -0.5)
    qi = jnp.arange(ATTN_BLOCK)[:, None]
    kj = jnp.arange(2 * ATTN_BLOCK)[None, :]
    dist = qi + ATTN_BLOCK - kj
    in_window = (dist >= 0) & (dist < WINDOW)
    key_pos = (jnp.arange(nb)[:, None, None] - 1) * ATTN_BLOCK + kj[None]
    mask = in_window[None] & (key_pos >= 0)
    scores = jnp.where(mask[None, :, None, None], scores, -jnp.inf)
    sink = sinks.astype(jnp.float32).reshape(N_KV_HEADS, Q_PER_KV)[None, None, :, :, None, None]
    sink = jnp.broadcast_to(sink, scores.shape[:-1] + (1,))
    probs = jax.nn.softmax(jnp.concatenate([scores, sink], axis=-1), axis=-1)[..., :-1]
    out = jnp.einsum('bnhgqk,bnkhd->bnqhgd', probs.astype(v.dtype), vb)
    return out.reshape(B, S, ATTN_WIDTH)


def spatial_gating(u, v, w_s, b_s, v_norm):
    B, S, _ = u.shape
    nc = S // CHUNK
    u = jax.nn.gelu(u)
    vg = rms_norm(jax.nn.gelu(v).reshape(B, S, SGU_GROUPS, SGU_GROUP_DIM), v_norm)
    vc = vg.reshape(B, nc, CHUNK, SGU_GROUPS, SGU_GROUP_DIM)
    w_causal = jnp.tril(w_s)
    s = jnp.einsum('gts,bnsgc->bntgc', w_causal, vc) + b_s.T[None, None, :, :, None]
    return u * s.reshape(B, S, SGU_WIDTH)


def causal_dwconv(x, w, b):
    C = x.shape[-1]
    y = lax.conv_general_dilated(
        x, w[:, None, :].astype(x.dtype), window_strides=(1,),
        padding=((CONV_WIDTH - 1, 0),), dimension_numbers=('NWC', 'WIO', 'NWC'),
        feature_group_count=C)
    return y + b


def setup_inputs(seed: int = 0) -> dict:
    key = jax.random.key(seed)
    ks = jax.random.split(key, 24)
    f32 = jnp.float32
    nrm = lambda k, shape, s: jax.random.normal(k, shape, f32) * s
    return {
        "x": nrm(ks[0], (BATCH, SEQ, D_MODEL), 1.0),
        "positions": (jnp.arange(SEQ, dtype=jnp.int32)[None, :]
                      + jax.random.randint(ks[1], (BATCH, 1), 0, SEQ, dtype=jnp.int32)),
        "norm1": 1.0 + nrm(ks[2], (DEPTH, D_MODEL), 0.02),
        "w_in": nrm(ks[3], (DEPTH, D_MODEL, IN_COLS), D_MODEL ** -0.5),
        "q_norm": 1.0 + nrm(ks[4], (DEPTH, HEAD_DIM), 0.02),
        "k_norm": 1.0 + nrm(ks[5], (DEPTH, HEAD_DIM), 0.02),
        "sinks": nrm(ks[6], (DEPTH, N_Q_HEADS), 0.5),
        "w_pool": nrm(ks[7], (DEPTH, POOL_GROUPS, POOL_GROUP_DIM, POOL_GROUP_DIM), POOL_GROUP_DIM ** -0.5),
        "pool_scale": 1.0 + nrm(ks[8], (DEPTH, POOL_WIDTH), 0.02),
        "sgu_v_norm": 1.0 + nrm(ks[9], (DEPTH, SGU_GROUP_DIM), 0.02),
        "w_s": nrm(ks[10], (DEPTH, SGU_GROUPS, CHUNK, CHUNK), CHUNK ** -0.5),
        "b_s": 1.0 + nrm(ks[11], (DEPTH, SGU_GROUPS, CHUNK), 0.02),
        "w_proj_a": nrm(ks[12], (DEPTH, POOL_WIDTH, D_MODEL), POOL_WIDTH ** -0.5),
        "w_proj_b": nrm(ks[13], (DEPTH, ATTN_WIDTH, D_MODEL), ATTN_WIDTH ** -0.5),
        "w_proj_c": nrm(ks[14], (DEPTH, SGU_WIDTH, D_MODEL), SGU_WIDTH ** -0.5),
        "w_out": nrm(ks[15], (DEPTH, D_MODEL, D_MODEL), D_MODEL ** -0.5),
        "norm2": 1.0 + nrm(ks[16], (DEPTH, D_MODEL), 0.02),
        "w_up": nrm(ks[17], (DEPTH, D_MODEL, 2 * D_FF), D_MODEL ** -0.5),
        "conv_w": nrm(ks[18], (DEPTH, CONV_WIDTH, 2 * D_FF), CONV_WIDTH ** -0.5),
        "conv_b": nrm(ks[19], (DEPTH, 2 * D_FF), 0.02),
        "w_down": nrm(ks[20], (DEPTH, D_FF, D_MODEL), D_FF ** -0.5),
    }


def reference(x, positions, norm1, w_in, q_norm, k_norm, sinks, w_pool, pool_scale,
              sgu_v_norm, w_s, b_s, w_proj_a, w_proj_b, w_proj_c, w_out, norm2,
              w_up, conv_w, conv_b, w_down):
    B, S, _ = x.shape
    cos, sin = rope_tables(positions)
    splits = np.cumsum([POOL_WIDTH, ATTN_WIDTH, KV_WIDTH, KV_WIDTH, SGU_WIDTH, SGU_WIDTH]).tolist()
    for l in range(DEPTH):
        h = rms_norm(x, norm1[l])
        z = h @ w_in[l]
        x_pool, q, k, v, u_s, v_s, gates = jnp.split(z, splits, axis=-1)

        y_a = pool_mixer(x_pool, w_pool[l], pool_scale[l]) @ w_proj_a[l]

        q = apply_rope(rms_norm(q.reshape(B, S, N_Q_HEADS, HEAD_DIM), q_norm[l]), cos, sin)
        k = apply_rope(rms_norm(k.reshape(B, S, N_KV_HEADS, HEAD_DIM), k_norm[l]), cos, sin)
        v = v.reshape(B, S, N_KV_HEADS, HEAD_DIM)
        y_b = swa_attention(q, k, v, sinks[l]) @ w_proj_b[l]

        y_c = spatial_gating(u_s, v_s, w_s[l], b_s[l], sgu_v_norm[l]) @ w_proj_c[l]

        g = jax.nn.sigmoid(gates.astype(jnp.float32)).astype(x.dtype).reshape(B, S, N_BRANCHES, D_MODEL)
        merged = g[:, :, 0] * y_a + g[:, :, 1] * y_b + g[:, :, 2] * y_c
        x = x + merged @ w_out[l]

        h = rms_norm(x, norm2[l])
        up = causal_dwconv(h @ w_up[l], conv_w[l], conv_b[l])
        gate, val = jnp.split(up, 2, axis=-1)
        x = x + (jax.nn.silu(gate) * val) @ w_down[l]
    return x
```

```python
import math
import numpy as np
from contextlib import ExitStack
import concourse.bass as bass
import concourse.mybir as mybir
from concourse.alu_op_type import AluOpType as ALU
from concourse.bass_utils import run_bass_kernel_spmd

AF = mybir.ActivationFunctionType
F32 = mybir.dt.float32
BF16 = mybir.dt.bfloat16
I32 = mybir.dt.int32
AX = mybir.AxisListType

ENG = ['pe', 'act', 'dve', 'pool', 'sp']
NCORES = 8
D = 1024
SEQ = 8192
OWN = 2048
HALO = 512
WIN = OWN + HALO
NSUB = WIN // 512
NBLK = WIN // 128
DFF = 2816
NJ = DFF // 128
EPS = 1e-6
NV = 8 + 8 + 2 + 44 + 132
NR = 64 * 3 + 8


class Sched:
    def __init__(self, nc, stack, self_sync=('act', 'dve', 'pool')):
        self.nc = nc
        self.stack = stack
        self.engs = {'pe': nc.tensor, 'act': nc.scalar, 'dve': nc.vector,
                     'pool': nc.gpsimd, 'sp': nc.sync}
        self.sem = {e: stack.enter_context(nc.semaphore("s_" + e)) for e in ENG}
        self.cnt = {e: 0 for e in ENG}
        self.known = {e: {} for e in ENG}
        self.snap = {}
        self.lastw = {}
        self.readers = {}
        self.dsem = {}
        self.dcnt = {}
        self.self_sync = set(self_sync)
        self.nwaits = 0
        self.nops = 0
        self.oplog = []

    def _handle(self, key):
        return self.sem[key] if key in self.sem else self.dsem[key]

    def _merge(self, e, key, val):
        k = self.known[e]
        if k.get(key, 0) < val:
            k[key] = val
        s = self.snap.get((key, val))
        if s:
            for kk, vv in s.items():
                if k.get(kk, 0) < vv:
                    k[kk] = vv

    def _deps(self, e, reads, writes):
        need = {}

        def add(kv):
            if kv is None:
                return
            key, val = kv
            if key == e and e not in self.self_sync:
                return
            if need.get(key, 0) < val:
                need[key] = val
        for r in reads:
            add(self.lastw.get(r))
        for r in writes:
            add(self.lastw.get(r))
            for kv in self.readers.get(r, {}).items():
                add(kv)
        eng = self.engs[e]
        for key, val in need.items():
            if self.known[e].get(key, 0) >= val:
                continue
            eng.wait_ge(self._handle(key), val)
            self.nwaits += 1
            self._merge(e, key, val)

    def _record(self, key, val, reads, writes):
        for r in reads:
            self.readers.setdefault(r, {})[key] = val
        for r in writes:
            self.lastw[r] = (key, val)
            self.readers[r] = {}

    limit = None

    def op(self, e, emit, reads=(), writes=()):
        if self.limit is not None and self.nops >= self.limit:
            return None
        self._deps(e, reads, writes)
        import sys as _sys
        self.oplog.append((self.nops, e, _sys._getframe(2).f_lineno, list(writes)))
        inst = emit(self.engs[e])
        self.cnt[e] += 1
        val = self.cnt[e]
        inst.then_inc(self.sem[e], 1)
        if e not in self.self_sync:
            self.known[e][e] = val
        self.snap[(e, val)] = dict(self.known[e])
        self._record(e, val, reads, writes)
        self.nops += 1
        return inst

    def dma(self, q, semname, out, in_, reads=(), writes=()):
        if self.limit is not None and self.nops >= self.limit:
            return None
        if semname not in self.dsem:
            self.dsem[semname] = self.stack.enter_context(self.nc.semaphore("d_" + semname))
            self.dcnt[semname] = 0
        self._deps(q, reads, writes)
        inst = self.engs[q].dma_start(out=out, in_=in_)
        self.dcnt[semname] += 16
        val = self.dcnt[semname]
        inst.then_inc(self.dsem[semname], 16)
        self.snap[(semname, val)] = dict(self.known[q])
        self._record(semname, val, reads, writes)
        return inst

    def wait_all(self, e):
        eng = self.engs[e]
        for k in ENG:
            if k != e and self.cnt[k] > 0:
                eng.wait_ge(self.sem[k], self.cnt[k])
        for k, v in self.dcnt.items():
            eng.wait_ge(self.dsem[k], v)


class Stream:
    def __init__(self, S, name, slots, seq):
        self.S = S
        self.name = name
        self.slots = slots
        self.seq = seq
        self.nload = 0
        self.nuse = 0
        for _ in range(len(slots)):
            self.load_next()

    def res(self, k):
        return "%s_%d" % (self.name, k)

    def load_next(self):
        if self.nload >= len(self.seq):
            return
        k = self.nload % len(self.slots)
        self.S.dma('pool', self.res(k), self.slots[k], self.seq[self.nload], writes=[self.res(k)])
        self.nload += 1

    def cur(self):
        k = self.nuse % len(self.slots)
        return self.slots[k], self.res(k)

    def done(self):
        self.nuse += 1
        self.load_next()


def build_program(nsub=NSUB, store_all=False, limit=None):
    nc = bass.Bass("TRN2", target_bir_lowering=False)

    def din(name, shape, dt=F32):
        return nc.dram_tensor(name, list(shape), dt, kind="ExternalInput").ap()

    xw = din("xw", [8, 128, WIN])
    posw = din("posw", [128, NBLK], I32)
    hv_d = din("hv", [128, 1])
    mask4_d = din("mask4", [128, 128])
    poolA4_d = din("poolA4", [128, 8, 128])
    maskc_d = din("maskc", [128, 128])
    maskp_d = din("maskp", [128, 128])
    poolA_d = din("poolA", [128, 8, 128])
    invf_d = din("invf", [128, 32])
    vecs_d = din("vecs", [2, 128, NV])
    rows_d = din("rows", [2, 128, NR])
    brep_d = din("brep", [2, 128, 2, 128])
    wsT_d = din("wsT", [2, 128, 4, 128])
    wpbd_d = din("wpbd", [2, 128, 2, 128])
    wtm_d = din("wtm", [2, 128, 8, 1280])
    wu_d = din("wu", [2, 128, 2, 8, 128])
    wgp_d = din("wgp", [2, 8, 128, 32, 128])
    wout_d = din("wout", [2, 8, 128, 8, 128])
    wup_d = din("wup", [2, NJ, 128, 2, 8, 128])
    wdn_d = din("wdn", [2, 8, 128, NJ, 128])
    out_d = nc.dram_tensor("out", [8, 128, OWN], F32, kind="ExternalOutput").ap()

    with ExitStack() as st:
        S = Sched(nc, st)
        S.limit = limit

        def sb(name, shape, dt):
            return st.enter_context(nc.sbuf_tensor(name, list(shape), dt))

        X = sb("X", [128, 2, 8, 512], F32)
        big = sb("big", [128, 22, 512], BF16)
        hT = sb("hT", [128, 8, 512], BF16)
        rs = sb("rs", [128, 512], F32)
        NF = 10
        Fr = sb("Fr", [128, NF, 640], F32)
        kT = sb("kT", [128, 2, 640], BF16)
        qT = sb("qT", [128, 4, 512], BF16)
        Vaug = sb("Vaug", [128, 2, 5, 2, 66], BF16)
        xab = sb("xab", [128, 2, 5, 256], BF16)
        vgb = sb("vgb", [128, 2, 256], BF16)
        qkn = sb("qkn", [128, 2, 640], BF16)
        ex = sb("ex", [128, 4, 512], BF16)
        PT = sb("PT", [128, 4, 512], BF16)
        attn_tm = sb("attn_tm", [128, 2, 512], BF16)
        sm = sb("sm", [128, 2, 48], F32)
        uT = sb("uT", [128, 2, 512], F32)
        diffT = sb("diffT", [128, 2, 512], BF16)
        aT = sb("aT", [128, 2, 512], BF16)
        tails = sb("tails", [128, 2, 44, 2], F32)
        ident = sb("ident", [128, 128], BF16)
        ones = sb("ones", [128, 128], BF16)
        maskc = sb("maskc_s", [128, 128], BF16)
        maskp = sb("maskp_s", [128, 128], BF16)
        mask4 = sb("mask4_s", [128, 128], BF16)
        poolA = sb("poolA_s", [128, 8, 128], BF16)
        poolA4 = sb("poolA4_s", [128, 8, 128], BF16)
        wsT = sb("wsT_s", [128, 2, 4, 128], BF16)
        wpbd = sb("wpbd_s", [128, 2, 2, 128], BF16)
        brep = sb("brep_s", [128, 2, 2, 128], F32)
        vecs = sb("vecs_s", [128, 2, NV], F32)
        rows = sb("rows_s", [128, 2, NR], F32)
        esink = sb("esink", [128, 2, 8], F32)
        hv = sb("hv_s", [128, 1], F32)
        cst = sb("cst", [128, 4], F32)
        invf = sb("invf_s", [128, 32], F32)
        posi = sb("posi", [128, NBLK], I32)
        posf = sb("posf", [128, NBLK], F32)
        cosT = sb("cosT", [128, NBLK, 32], F32)
        sinT = sb("sinT", [128, NBLK, 32], F32)
        Wtm = sb("Wtm", [128, 8, 1280], BF16)
        Wu = sb("Wu", [128, 2, 8, 128], BF16)
        Wgp = sb("Wgp", [128, 2, 32, 128], BF16)
        Wout = sb("Wout", [128, 2, 8, 128], BF16)
        Wup = sb("Wup", [128, 2, 2, 8, 128], BF16)
        Wdn = sb("Wdn", [128, 2, NJ, 128], BF16)
        ps = st.enter_context(nc.psum_tensor("ps", [128, 8, 512], F32))

        state = {'bank': 0, 'f': 0}

        def bank():
            b = state['bank']
            state['bank'] = (b + 1) % 8
            return b, "ps%d" % b

        def fslot():
            f = state['f']
            state['f'] = (f + 1) % NF
            return Fr[:, f, :], "F%d" % f

        def mm_group(out_ap, pairs, reads, writes):
            n = len(pairs)

            def emit(e):
                inst = None
                for i, (l, r) in enumerate(pairs):
                    inst = e.matmul(out_ap, lhsT=l, rhs=r, start=(i == 0), stop=(i == n - 1))
                return inst
            S.op('pe', emit, reads=reads, writes=writes)

        def OP(e, fn, reads, writes):
            S.op(e, fn, reads=reads, writes=writes)

        order = [(s, l) for s in range(nsub) for l in range(2)]
        st_wtm = Stream(S, "Wtm", [Wtm[:]], [wtm_d[l] for (s, l) in order])
        st_wu = Stream(S, "Wu", [Wu[:]], [wu_d[l] for (s, l) in order])
        st_wgp = Stream(S, "Wgp", [Wgp[:, k] for k in range(2)], [wgp_d[l, c] for (s, l) in order for c in range(8)])
        st_wout = Stream(S, "Wout", [Wout[:, k] for k in range(2)], [wout_d[l, c] for (s, l) in order for c in range(8)])
        st_wup = Stream(S, "Wup", [Wup[:, k] for k in range(2)], [wup_d[l, j] for (s, l) in order for j in range(NJ)])
        st_wdn = Stream(S, "Wdn", [Wdn[:, k] for k in range(2)], [wdn_d[l, c] for (s, l) in order for c in range(8)])

        S.dma('pool', 'c_maskc', maskc[:], maskc_d, writes=['maskc'])
        S.dma('pool', 'c_maskp', maskp[:], maskp_d, writes=['maskp'])
        S.dma('pool', 'c_mask4', mask4[:], mask4_d, writes=['mask4'])
        S.dma('pool', 'c_poolA', poolA[:], poolA_d, writes=['poolA'])
        S.dma('pool', 'c_poolA4', poolA4[:], poolA4_d, writes=['poolA4'])
        for l in range(2):
            S.dma('pool', 'c_wsT%d' % l, wsT[:, l], wsT_d[l], writes=['wsT%d' % l])
            S.dma('pool', 'c_wpbd%d' % l, wpbd[:, l], wpbd_d[l], writes=['wpbd%d' % l])
            S.dma('sp', 'c_brep%d' % l, brep[:, l], brep_d[l], writes=['brep%d' % l])
            S.dma('sp', 'c_vecs%d' % l, vecs[:, l], vecs_d[l], writes=['vecs'])
            S.dma('sp', 'c_rows%d' % l, rows[:, l], rows_d[l], writes=['rows'])
        S.dma('sp', 'c_hv', hv[:], hv_d, writes=['hv'])
        S.dma('sp', 'c_invf', invf[:], invf_d, writes=['invf'])
        S.dma('sp', 'c_pos', posi[:], posw, writes=['posi'])

        OP('pool', lambda e: e.memset(ident[:], 0.0), [], ['ident'])
        OP('pool', lambda e: e.affine_select(out=ident[:], in_=ident[:], pattern=[[-1, 128]],
                                             compare_op=ALU.not_equal, fill=1.0, base=0,
                                             channel_multiplier=1), ['ident'], ['ident'])
        OP('dve', lambda e: e.memset(ones[:], 1.0), [], ['ones'])
        OP('dve', lambda e: e.memset(cst[:, 0:1], -0.5), [], ['cst'])
        OP('dve', lambda e: e.memset(tails[:], 0.0), [], ['tails0', 'tails1'])
        OP('dve', lambda e: e.memset(kT[:], 0.0), [], ['kT0', 'kT1'])
        OP('dve', lambda e: e.memset(Vaug[:], 1.0), [], ['Vaug0', 'Vaug1'])
        OP('dve', lambda e: e.memset(xab[:], 0.0), [], ['xab0', 'xab1'])
        for l in range(2):
            OP('dve', lambda e, l=l: e.tensor_tensor(out=wsT[:, l], in0=wsT[:, l],
                                                     in1=maskc[:].unsqueeze(1).broadcast_to([128, 4, 128]), op=ALU.mult),
               ['wsT%d' % l, 'maskc'], ['wsT%d' % l])
            OP('act', lambda e, l=l: e.activation(out=esink[:, l, :], in_=rows[:, l, 192:200], func=AF.Exp),
               ['rows'], ['esink'])

        OP('dve', lambda e: e.tensor_copy(out=posf[:], in_=posi[:]), ['posi'], ['posf'])
        ang, angr = fslot()
        ang = ang[:, 0:NBLK * 32].rearrange("p (a b) -> p a b", a=NBLK)
        OP('dve', lambda e: e.tensor_tensor(out=ang, in0=posf[:].unsqueeze(2).broadcast_to([128, NBLK, 32]),
                                            in1=invf[:].unsqueeze(1).broadcast_to([128, NBLK, 32]), op=ALU.mult),
           ['posf', 'invf'], [angr])
        MAGIC = 12582912.0
        TWO_PI = 2.0 * math.pi
        C1 = 6.28125
        C2 = float(np.float32(TWO_PI - C1))
        C3 = float(TWO_PI - C1 - C2)
        tk, tkr = fslot()
        tk = tk[:, 0:NBLK * 32].rearrange("p (a b) -> p a b", a=NBLK)
        rr, rrr = fslot()
        rr = rr[:, 0:NBLK * 32].rearrange("p (a b) -> p a b", a=NBLK)
        r2, r2r = fslot()
        r2 = r2[:, 0:NBLK * 32].rearrange("p (a b) -> p a b", a=NBLK)
        OP('dve', lambda e: e.tensor_scalar(out=tk, in0=ang, scalar1=1.0 / TWO_PI, scalar2=MAGIC, op0=ALU.mult, op1=ALU.add), [angr], [tkr])
        OP('dve', lambda e: e.tensor_scalar(out=tk, in0=tk, scalar1=-MAGIC, scalar2=None, op0=ALU.add), [tkr], [tkr])
        OP('dve', lambda e: e.scalar_tensor_tensor(out=rr, in0=tk, scalar=-C1, in1=ang, op0=ALU.mult, op1=ALU.add), [tkr, angr], [rrr])
        OP('dve', lambda e: e.scalar_tensor_tensor(out=rr, in0=tk, scalar=-C2, in1=rr, op0=ALU.mult, op1=ALU.add), [tkr, rrr], [rrr])
        OP('dve', lambda e: e.scalar_tensor_tensor(out=rr, in0=tk, scalar=-C3, in1=rr, op0=ALU.mult, op1=ALU.add), [tkr, rrr], [rrr])
        OP('dve', lambda e: e.tensor_scalar(out=r2, in0=rr, scalar1=math.pi / 2, scalar2=None, op0=ALU.add), [rrr], [r2r])
        OP('dve', lambda e: e.tensor_scalar(out=tk, in0=r2, scalar1=math.pi, scalar2=-TWO_PI, op0=ALU.is_gt, op1=ALU.mult), [r2r], [tkr])
        OP('dve', lambda e: e.tensor_tensor(out=r2, in0=r2, in1=tk, op=ALU.add), [r2r, tkr], [r2r])
        PI_SAFE = 3.1415925
        OP('dve', lambda e: e.tensor_scalar(out=rr, in0=rr, scalar1=PI_SAFE, scalar2=-PI_SAFE, op0=ALU.min, op1=ALU.max), [rrr], [rrr])
        OP('dve', lambda e: e.tensor_scalar(out=r2, in0=r2, scalar1=PI_SAFE, scalar2=-PI_SAFE, op0=ALU.min, op1=ALU.max), [r2r], [r2r])
        OP('act', lambda e: e.activation(out=sinT[:], in_=rr, func=AF.Sin), [rrr], ['sinT'])
        OP('act', lambda e: e.activation(out=cosT[:], in_=r2, func=AF.Sin), [r2r], ['cosT'])

        def load_x(s):
            xb = s % 2
            for c in range(8):
                S.dma('sp', 'x%d_%d' % (xb, c), X[:, xb, c, :], xw[c, :, s * 512:(s + 1) * 512],
                      writes=['X%d_%d' % (xb, c)])

        def rmsnorm(xb, l, gcol):
            for c in range(8):
                OP('act', lambda e, c=c: e.activation(out=big[:, 8 + c, :], in_=X[:, xb, c, :], func=AF.Square),
                   ['X%d_%d' % (xb, c)], ['big%d' % (8 + c)])
            b, br = bank()
            mm_group(ps[:, b, :], [(ones[:], big[:, 8 + c, :]) for c in range(8)],
                     ['ones'] + ['big%d' % (8 + c) for c in range(8)], [br])
            OP('dve', lambda e: e.tensor_scalar(out=rs[:], in0=ps[:, b, :], scalar1=1.0 / D, scalar2=EPS, op0=ALU.mult, op1=ALU.add),
               [br], ['rs'])
            OP('pool', lambda e: e.tensor_tensor(out=rs[:], in0=rs[:], in1=cst[:, 0:1].broadcast_to([128, 512]), op=ALU.pow),
               ['rs', 'cst'], ['rs'])
            for c in range(8):
                OP('dve', lambda e, c=c: e.scalar_tensor_tensor(out=hT[:, c, :], in0=X[:, xb, c, :], scalar=vecs[:, l, gcol + c:gcol + c + 1],
                                                               in1=rs[:], op0=ALU.mult, op1=ALU.mult),
                   ['X%d_%d' % (xb, c), 'rs', 'vecs'], ['hT%d' % c])

        HT = ['hT%d' % c for c in range(8)]

        def gelu2(src_ap, src_res, n, dst_ap, dst_res):
            t1, t1r = fslot()
            t1 = t1[:, 0:n]
            OP('act', lambda e: e.activation(out=t1, in_=src_ap, func=AF.Square), [src_res], [t1r])
            OP('dve', lambda e: e.tensor_scalar(out=t1, in0=t1, scalar1=0.044715, scalar2=1.0, op0=ALU.mult, op1=ALU.add), [t1r], [t1r])
            OP('dve', lambda e: e.tensor_tensor(out=t1, in0=t1, in1=src_ap, op=ALU.mult), [t1r, src_res], [t1r])
            OP('act', lambda e: e.activation(out=t1, in_=t1, func=AF.Tanh, scale=0.7978845608028654), [t1r], [t1r])
            OP('dve', lambda e: e.scalar_tensor_tensor(out=dst_ap, in0=t1, scalar=1.0, in1=src_ap, op0=ALU.add, op1=ALU.mult),
               [t1r, src_res], [dst_res])

        def mixer(s, l):
            xb = s % 2
            rmsnorm(xb, l, 0)
            w, wr = st_wu.cur()
            for cc in range(2):
                b, br = bank()
                mm_group(ps[:, b, :], [(w[:, cc, kc, :], hT[:, kc, :]) for kc in range(8)], [wr] + HT, [br])
                gelu2(ps[:, b, :], br, 512, uT[:, cc, :], 'uT%d' % cc)
            st_wu.done()
            wtm, wtmr = st_wtm.cur()
            KT = 'kT%d' % l
            OP('dve', lambda e: e.tensor_copy(out=kT[:, l, 0:128], in_=kT[:, l, 512:640]), [KT], [KT])
            OP('dve', lambda e: e.tensor_copy(out=Vaug[:, l, 0], in_=Vaug[:, l, 4]), ['Vaug%d' % l], ['Vaug%d' % l])
            OP('dve', lambda e: e.tensor_copy(out=xab[:, l, 0], in_=xab[:, l, 4]), ['xab%d' % l], ['xab%d' % l])
            for bi in range(4):
                blk = s * 4 + bi
                tb = bi * 128
                par = bi % 2
                bA, bAr = bank()
                bB, bBr = bank()
                bC, bCr = bank()
                mm_group(ps[:, bA, :], [(hT[:, kc, tb:tb + 128], wtm[:, kc, 0:512]) for kc in range(8)], [wtmr] + HT, [bAr])
                mm_group(ps[:, bB, :], [(hT[:, kc, tb:tb + 128], wtm[:, kc, 512:1024]) for kc in range(8)], [wtmr] + HT, [bBr])
                mm_group(ps[:, bC, 0:256], [(hT[:, kc, tb:tb + 128], wtm[:, kc, 1024:1280]) for kc in range(8)], [wtmr] + HT, [bCr])
                if bi == 3:
                    st_wtm.done()
                SM = 'sm%d' % par
                ss = sm[:, par, 0:14]
                rq = sm[:, par, 16:30]
                t1, t1r = fslot()
                OP('act', lambda e: e.activation(out=t1[:, 0:512], in_=ps[:, bA, :], func=AF.Square), [bAr], [t1r])
                OP('act', lambda e: e.activation(out=t1[:, 512:640], in_=ps[:, bB, 0:128], func=AF.Square), [bBr, t1r], [t1r])
                OP('dve', lambda e: e.tensor_reduce(out=ss[:, 0:10], in_=t1[:, 0:640].rearrange("p (a b) -> p a b", a=10), axis=AX.X, op=ALU.add),
                   [t1r], [SM])
                gl, glr = fslot()
                gelu2(ps[:, bB, 256:512], bBr, 256, gl[:, 0:256], glr)
                OP('act', lambda e: e.activation(out=gl[:, 256:512], in_=gl[:, 0:256], func=AF.Square, scale=0.5), [glr], [glr])
                OP('dve', lambda e: e.tensor_reduce(out=ss[:, 10:14], in_=gl[:, 256:512].rearrange("p (a b) -> p a b", a=4), axis=AX.X, op=ALU.add),
                   [glr, SM], [SM])
                OP('dve', lambda e: e.tensor_scalar(out=rq, in0=ss, scalar1=1.0 / 64, scalar2=EPS, op0=ALU.mult, op1=ALU.add), [SM], [SM])
                OP('pool', lambda e: e.tensor_tensor(out=rq, in0=rq, in1=cst[:, 0:1].broadcast_to([128, 14]), op=ALU.pow), [SM, 'cst'], [SM])
                qkg, qkgr = fslot()
                OP('dve', lambda e: e.tensor_tensor(out=qkg[:, 0:512].rearrange("p (a b) -> p a b", a=8),
                                                    in0=ps[:, bA, :].rearrange("p (a b) -> p a b", a=8),
                                                    in1=rows[:, l, 0:64].unsqueeze(1).broadcast_to([128, 8, 64]), op=ALU.mult),
                   [bAr, 'rows'], [qkgr])
                OP('dve', lambda e: e.tensor_tensor(out=qkg[:, 512:640].rearrange("p (a b) -> p a b", a=2),
                                                    in0=ps[:, bB, 0:128].rearrange("p (a b) -> p a b", a=2),
                                                    in1=rows[:, l, 64:128].unsqueeze(1).broadcast_to([128, 2, 64]), op=ALU.mult),
                   [bBr, 'rows', qkgr], [qkgr])
                H = qkg[:, 0:640].rearrange("p (a h b) -> p a h b", a=10, h=2)
                t1v, t2v = H[:, :, 0, :], H[:, :, 1, :]
                cb_ = cosT[:, blk, :].unsqueeze(1).broadcast_to([128, 10, 32])
                sb_ = sinT[:, blk, :].unsqueeze(1).broadcast_to([128, 10, 32])
                ta, tar = fslot()
                A_ = ta[:, 0:320].rearrange("p (a b) -> p a b", a=10)
                B_ = ta[:, 320:640].rearrange("p (a b) -> p a b", a=10)
                tc_, tcr = fslot()
                C_ = tc_[:, 0:320].rearrange("p (a b) -> p a b", a=10)
                D_ = tc_[:, 320:640].rearrange("p (a b) -> p a b", a=10)
                ro, ror = fslot()
                RO = ro[:, 0:640].rearrange("p (a h b) -> p a h b", a=10, h=2)
                OP('dve', lambda e: e.tensor_tensor(out=A_, in0=t1v, in1=cb_, op=ALU.mult), [qkgr, 'cosT'], [tar])
                OP('dve', lambda e: e.tensor_tensor(out=B_, in0=t2v, in1=sb_, op=ALU.mult), [qkgr, 'sinT', tar], [tar])
                OP('dve', lambda e: e.tensor_tensor(out=C_, in0=t2v, in1=cb_, op=ALU.mult), [qkgr, 'cosT'], [tcr])
                OP('dve', lambda e: e.tensor_tensor(out=D_, in0=t1v, in1=sb_, op=ALU.mult), [qkgr, 'sinT', tcr], [tcr])
                OP('dve', lambda e: e.tensor_tensor(out=RO[:, :, 0, :], in0=A_, in1=B_, op=ALU.subtract), [tar], [ror])
                OP('dve', lambda e: e.tensor_tensor(out=RO[:, :, 1, :], in0=C_, in1=D_, op=ALU.add), [tcr, ror], [ror])
                QKN = 'qkn%d' % par
                OP('dve', lambda e: e.tensor_tensor(out=qkn[:, par, :].rearrange("p (a b) -> p a b", a=10),
                                                    in0=ro[:, 0:640].rearrange("p (a b) -> p a b", a=10),
                                                    in1=rq[:, 0:10].unsqueeze(2).broadcast_to([128, 10, 64]), op=ALU.mult),
                   [ror, SM], [QKN])
                VA = 'Vaug%d' % l
                OP('act', lambda e: e.copy(out=Vaug[:, l, bi + 1, :, 0:64], in_=ps[:, bB, 128:256].rearrange("p (a b) -> p a b", a=2)),
                   [bBr], [VA])
                OP('dve', lambda e: e.scalar_tensor_tensor(out=gl[:, 256:512].rearrange("p (a b) -> p a b", a=4),
                                                           in0=gl[:, 0:256].rearrange("p (a b) -> p a b", a=4), scalar=0.5,
                                                           in1=rows[:, l, 128:192].unsqueeze(1).broadcast_to([128, 4, 64]),
                                                           op0=ALU.mult, op1=ALU.mult), [glr, 'rows'], [glr])
                VG = 'vgb%d' % par
                OP('dve', lambda e: e.tensor_tensor(out=vgb[:, par, :].rearrange("p (a b) -> p a b", a=4),
                                                    in0=gl[:, 256:512].rearrange("p (a b) -> p a b", a=4),
                                                    in1=rq[:, 10:14].unsqueeze(2).broadcast_to([128, 4, 64]), op=ALU.mult),
                   [glr, SM], [VG])
                XA = 'xab%d' % l
                OP('act', lambda e: e.copy(out=xab[:, l, bi + 1, :], in_=ps[:, bC, 0:256]), [bCr], [XA])
                bT, bTr = bank()
                psb = ps[:, bT, :].bitcast(BF16)

                def emit_tr(e):
                    inst = None
                    for i in range(5):
                        inst = e.transpose(out=psb[:, i * 128:(i + 1) * 128], in_=qkn[:, par, i * 128:(i + 1) * 128], identity=ident[:])
                    return inst
                OP('pe', emit_tr, [QKN, 'ident'], [bTr])
                OP('act', lambda e: e.copy(out=qT[:, :, tb:tb + 128], in_=psb[:, 0:512].rearrange("p (a b) -> p a b", a=4)), [bTr], ['qT'])
                OP('act', lambda e: e.copy(out=kT[:, l, 128 + tb:256 + tb], in_=psb[:, 512:640]), [bTr], [KT])
                pvsrc = []
                for kb in range(2):
                    koff = tb + kb * 128
                    if kb == 0:
                        mk, mkr = (mask4, 'mask4') if blk == 4 else (maskp, 'maskp')
                    else:
                        mk, mkr = maskc, 'maskc'
                    for grp in range(2):
                        idx = kb * 2 + grp
                        bS, bSr = bank()
                        OP('pe', lambda e, bS=bS, grp=grp, koff=koff: e.matmul(
                            ps[:, bS, :].rearrange("p (a b) -> p a b", a=4),
                            lhsT=kT[grp * 64:(grp + 1) * 64, l, koff:koff + 128],
                            rhs=qT[grp * 64:(grp + 1) * 64, :, tb:tb + 128], start=True, stop=True),
                           [KT, 'qT'], [bSr])
                        OP('act', lambda e, bS=bS, idx=idx: e.activation(out=ex[:, idx, :], in_=ps[:, bS, :], func=AF.Exp, scale=0.125),
                           [bSr], ['ex%d' % idx])
                        OP('dve', lambda e, idx=idx, mk=mk: e.tensor_tensor(
                            out=PT[:, idx, :].rearrange("p (a b) -> p a b", a=4),
                            in0=ex[:, idx, :].rearrange("p (a b) -> p a b", a=4),
                            in1=mk[:].unsqueeze(1).broadcast_to([128, 4, 128]), op=ALU.mult),
                           ['ex%d' % idx, mkr], ['PT%d' % idx])
                obanks = []
                for grp in range(2):
                    bO, bOr = bank()
                    obanks.append((bO, bOr))

                    def emit_pv(e, grp=grp, bO=bO):
                        inst = None
                        for c in range(4):
                            for kb in range(2):
                                inst = e.matmul(ps[:, bO, c * 128:c * 128 + 65],
                                                lhsT=PT[:, kb * 2 + grp, c * 128:(c + 1) * 128],
                                                rhs=Vaug[:, l, bi + kb, grp, 0:65], start=(kb == 0), stop=(kb == 1))
                        return inst
                    OP('pe', emit_pv, ['PT%d' % grp, 'PT%d' % (2 + grp), VA], [bOr])
                den = sm[:, par, 32:40]
                for grp in range(2):
                    bO, bOr = obanks[grp]
                    OP('dve', lambda e, grp=grp, bO=bO: e.tensor_tensor(
                        out=den[:, grp * 4:(grp + 1) * 4],
                        in0=ps[:, bO, :].rearrange("p (a b) -> p a b", a=4)[:, :, 64],
                        in1=esink[:, l, grp * 4:(grp + 1) * 4], op=ALU.add), [bOr, 'esink', SM], [SM])
                OP('dve', lambda e: e.reciprocal(out=den, in_=den), [SM], [SM])
                AT = 'attn_tm%d' % par
                for grp in range(2):
                    bO, bOr = obanks[grp]
                    OP('dve', lambda e, grp=grp, bO=bO: e.tensor_tensor(
                        out=attn_tm[:, par, grp * 256:(grp + 1) * 256].rearrange("p (a b) -> p a b", a=4),
                        in0=ps[:, bO, :].rearrange("p (a b) -> p a b", a=4)[:, :, 0:64],
                        in1=den[:, grp * 4:(grp + 1) * 4].unsqueeze(2).broadcast_to([128, 4, 64]), op=ALU.mult),
                       [bOr, SM], [AT])
                bT2, bT2r = bank()
                psb2 = ps[:, bT2, :].bitcast(BF16)

                def emit_tr2(e):
                    inst = None
                    for i in range(4):
                        inst = e.transpose(out=psb2[:, i * 128:(i + 1) * 128], in_=attn_tm[:, par, i * 128:(i + 1) * 128], identity=ident[:])
                    return inst
                OP('pe', emit_tr2, [AT, 'ident'], [bT2r])
                OP('act', lambda e: e.copy(out=big[:, 16:20, tb:tb + 128], in_=psb2[:, 0:512].rearrange("p (a b) -> p a b", a=4)),
                   [bT2r], ['big16', 'big17', 'big18', 'big19'])
                bG, bGr = bank()

                def emit_sgu(e):
                    inst = None
                    for g in range(4):
                        cc = g // 2
                        inst = e.matmul(ps[:, bG, g * 128:(g + 1) * 128], lhsT=vgb[:, par, cc * 128:(cc + 1) * 128],
                                        rhs=wsT[:, l, g, :], start=True, stop=True)
                    return inst
                OP('pe', emit_sgu, [VG, 'wsT%d' % l], [bGr])
                for g in range(4):
                    cc, h = g // 2, g % 2
                    pr = slice(h * 64, (h + 1) * 64)
                    tt, ttr = fslot()
                    OP('dve', lambda e, g=g, cc=cc, pr=pr, tt=tt: e.tensor_tensor(out=tt[pr, 0:128], in0=ps[pr, bG, g * 128:(g + 1) * 128],
                                                                              in1=brep[pr, l, cc, :], op=ALU.add),
                       [bGr, 'brep%d' % l], [ttr])
                    OP('dve', lambda e, cc=cc, pr=pr, tt=tt: e.scalar_tensor_tensor(out=big[pr, 20 + cc, tb:tb + 128], in0=tt[pr, 0:128], scalar=0.5,
                                                                                   in1=uT[pr, cc, tb:tb + 128], op0=ALU.mult, op1=ALU.mult),
                       [ttr, 'uT%d' % cc], ['big%d' % (20 + cc)])
                bP, bPr = bank()
                pA = poolA4 if blk == 4 else poolA
                pAr = 'poolA4' if blk == 4 else 'poolA'

                def emit_pool(e):
                    inst = None
                    for g in range(4):
                        cc = g // 2
                        inst = e.matmul(ps[:, bP, g * 128:(g + 1) * 128], lhsT=xab[:, l, bi + 1, cc * 128:(cc + 1) * 128],
                                        rhs=pA[:, g * 2, :], start=True, stop=False)
                        inst = e.matmul(ps[:, bP, g * 128:(g + 1) * 128], lhsT=xab[:, l, bi, cc * 128:(cc + 1) * 128],
                                        rhs=pA[:, g * 2 + 1, :], start=False, stop=True)
                    return inst
                OP('pe', emit_pool, [XA, pAr], [bPr])
                for g in range(4):
                    cc, h = g // 2, g % 2
                    pr = slice(h * 64, (h + 1) * 64)
                    OP('act', lambda e, g=g, cc=cc, pr=pr: e.copy(out=diffT[pr, cc, tb:tb + 128], in_=ps[pr, bP, g * 128:(g + 1) * 128]),
                       [bPr], ['diffT%d' % cc])
            for cc in range(2):
                b, br = bank()
                mm_group(ps[:, b, :], [(wpbd[:, l, cc, :], diffT[:, cc, :])], ['wpbd%d' % l, 'diffT%d' % cc], [br])
                OP('act', lambda e, cc=cc, b=b: e.activation(out=aT[:, cc, :], in_=ps[:, b, :], func=AF.Identity, scale=vecs[:, l, 16 + cc:17 + cc]),
                   [br, 'vecs'], ['aT%d' % cc])
            srcs = [(aT, ['aT0', 'aT1'], 24, 2, None), (big, ['big16', 'big17', 'big18', 'big19'], 26, 4, 16), (big, ['big20', 'big21'], 30, 2, 20)]
            for c in range(8):
                w, wr = st_wgp.cur()
                gb = []
                for i in range(3):
                    b, br = bank()
                    mm_group(ps[:, b, :], [(w[:, i * 8 + kc, :], hT[:, kc, :]) for kc in range(8)], [wr] + HT, [br])
                    gb.append((b, br))
                yb = []
                for (buf, rres, w0, nk, off) in srcs:
                    b, br = bank()
                    if off is None:
                        pairs = [(w[:, w0 + kk, :], buf[:, kk, :]) for kk in range(nk)]
                    else:
                        pairs = [(w[:, w0 + kk, :], buf[:, off + kk, :]) for kk in range(nk)]
                    mm_group(ps[:, b, :], pairs, [wr] + rres, [br])
                    yb.append((b, br))
                st_wgp.done()
                tts = []
                for i in range(3):
                    th, thr = fslot()
                    OP('act', lambda e, i=i, th=th: e.activation(out=th[:, 0:512], in_=ps[:, gb[i][0], :], func=AF.Tanh, scale=0.5), [gb[i][1]], [thr])
                    OP('dve', lambda e, i=i, th=th: e.scalar_tensor_tensor(out=th[:, 0:512], in0=th[:, 0:512], scalar=1.0, in1=ps[:, yb[i][0], :],
                                                                           op0=ALU.add, op1=ALU.mult), [thr, yb[i][1]], [thr])
                    tts.append((th, thr))
                OP('dve', lambda e: e.tensor_tensor(out=tts[0][0][:, 0:512], in0=tts[0][0][:, 0:512], in1=tts[1][0][:, 0:512], op=ALU.add),
                   [tts[0][1], tts[1][1]], [tts[0][1]])
                OP('dve', lambda e, c=c: e.tensor_tensor(out=big[:, 8 + c, :], in0=tts[0][0][:, 0:512], in1=tts[2][0][:, 0:512], op=ALU.add),
                   [tts[0][1], tts[2][1]], ['big%d' % (8 + c)])
            for c in range(8):
                w, wr = st_wout.cur()
                b, br = bank()
                mm_group(ps[:, b, :], [(w[:, kc, :], big[:, 8 + kc, :]) for kc in range(8)], [wr] + ['big%d' % (8 + kc) for kc in range(8)], [br])
                st_wout.done()
                OP('dve', lambda e, c=c, b=b: e.scalar_tensor_tensor(out=X[:, xb, c, :], in0=ps[:, b, :], scalar=0.5, in1=X[:, xb, c, :],
                                                                     op0=ALU.mult, op1=ALU.add), [br, 'X%d_%d' % (xb, c)], ['X%d_%d' % (xb, c)])

        def ffn(s, l, store):
            xb = s % 2
            rmsnorm(xb, l, 8)
            TL = 'tails%d' % l
            for j in range(NJ):
                w, wr = st_wup.cur()
                accs = []
                pbs = []
                for gv in range(2):
                    b, br = bank()
                    mm_group(ps[:, b, :], [(w[:, gv, kc, :], hT[:, kc, :]) for kc in range(8)], [wr] + HT, [br])
                    pbs.append((b, br))
                st_wup.done()
                for gv in range(2):
                    b, br = pbs[gv]
                    ch = gv * NJ + j
                    ub, ubr = fslot()
                    acc, accr = fslot()
                    OP('pool', lambda e, ub=ub, ch=ch: e.tensor_copy(out=ub[:, 0:2], in_=tails[:, l, ch, :]), [TL], [ubr])
                    OP('act', lambda e, ub=ub, b=b: e.copy(out=ub[:, 2:514], in_=ps[:, b, :]), [br, ubr], [ubr])
                    OP('act', lambda e, acc=acc, b=b, ch=ch: e.activation(out=acc[:, 0:512], in_=ps[:, b, :], func=AF.Identity,
                                                                        scale=vecs[:, l, 62 + ch * 3 + 2:62 + ch * 3 + 3],
                                                                        bias=vecs[:, l, 18 + ch:19 + ch]), [br, 'vecs'], [accr])
                    OP('dve', lambda e, acc=acc, ub=ub, ch=ch: e.scalar_tensor_tensor(out=acc[:, 0:512], in0=ub[:, 1:513],
                                                                                     scalar=vecs[:, l, 62 + ch * 3 + 1:62 + ch * 3 + 2],
                                                                                     in1=acc[:, 0:512], op0=ALU.mult, op1=ALU.add),
                       [ubr, accr, 'vecs'], [accr])
                    OP('dve', lambda e, acc=acc, ub=ub, ch=ch: e.scalar_tensor_tensor(out=acc[:, 0:512], in0=ub[:, 0:512],
                                                                                     scalar=vecs[:, l, 62 + ch * 3:62 + ch * 3 + 1],
                                                                                     in1=acc[:, 0:512], op0=ALU.mult, op1=ALU.add),
                       [ubr, accr, 'vecs'], [accr])
                    OP('pool', lambda e, ub=ub, ch=ch: e.tensor_copy(out=tails[:, l, ch, :], in_=ub[:, 512:514]), [ubr, TL], [TL])
                    accs.append((acc, accr))
                OP('act', lambda e: e.activation(out=accs[0][0][:, 0:512], in_=accs[0][0][:, 0:512], func=AF.Silu), [accs[0][1]], [accs[0][1]])
                OP('dve', lambda e, j=j: e.tensor_tensor(out=big[:, j, :], in0=accs[0][0][:, 0:512], in1=accs[1][0][:, 0:512], op=ALU.mult),
                   [accs[0][1], accs[1][1]], ['big%d' % j])
            if s == 0:
                OP('dve', lambda e: e.tensor_scalar(out=tails[:, l], in0=tails[:, l], scalar1=hv[:, 0:1], scalar2=None, op0=ALU.mult),
                   [TL, 'hv'], [TL])
            for c in range(8):
                w, wr = st_wdn.cur()
                b, br = bank()
                mm_group(ps[:, b, :], [(w[:, j, :], big[:, j, :]) for j in range(NJ)], [wr] + ['big%d' % j for j in range(NJ)], [br])
                st_wdn.done()
                OP('dve', lambda e, c=c, b=b: e.tensor_tensor(out=X[:, xb, c, :], in0=ps[:, b, :], in1=X[:, xb, c, :], op=ALU.add),
                   [br, 'X%d_%d' % (xb, c)], ['X%d_%d' % (xb, c)])
                if store:
                    so = s if store_all else s - 1
                    S.dma('sp', 'o%d_%d' % (xb, c), out_d[c, :, so * 512:(so + 1) * 512], X[:, xb, c, :], reads=['X%d_%d' % (xb, c)])

        load_x(0)
        for s in range(nsub):
            if s + 1 < nsub:
                load_x(s + 1)
            for l in range(2):
                mixer(s, l)
                ffn(s, l, store=(l == 1 and (s >= 1 or store_all)))
        S.wait_all('sp')
        print("sched: ops=%d waits=%d" % (S.nops, S.nwaits))
        build_program.oplog = S.oplog
    return nc


def _consts():
    i = np.arange(128)
    maskc = (i[:, None] <= i[None, :]).astype(np.float32)
    maskp = (i[:, None] > i[None, :]).astype(np.float32)
    windows = (2, 4, 8, 16)

    def poolmats(first):
        A = np.zeros((128, 8, 128), np.float32)
        for g, w in enumerate(windows):
            for t in range(128):
                cnt = min(t + 1, w) if first else w
                for k in range(w):
                    sidx = t - k
                    if sidx >= 0:
                        A[sidx, g * 2, t] += 1.0 / cnt
                    elif not first:
                        A[128 + sidx, g * 2 + 1, t] += 1.0 / cnt
                A[t, g * 2, t] -= 1.0
        return A
    invf = (10000.0 ** (-np.arange(0, 64, 2, dtype=np.float32) / 64)).astype(np.float32)
    return maskc, maskp, poolmats(False), poolmats(True), np.broadcast_to(invf, (128, 32)).copy()


def _prep_weights(norm1, w_in, q_norm, k_norm, sinks, w_pool, pool_scale, sgu_v_norm, w_s, b_s,
                  w_proj_a, w_proj_b, w_proj_c, w_out, norm2, w_up, conv_w, conv_b, w_down):
    f = np.float32
    L = 2
    qcols = np.empty(512, np.int64)
    for c in range(4):
        for half in range(2):
            h = c + 4 * half
            qcols[c * 128 + half * 64:(c * 128 + half * 64 + 64)] = 256 + h * 64 + np.arange(64)
    tmcols = np.concatenate([qcols, np.arange(768, 896), np.arange(896, 1024), np.arange(1280, 1536), np.arange(0, 256)])
    wtm = np.ascontiguousarray(w_in[:, :, tmcols].reshape(L, 8, 128, 1280).transpose(0, 2, 1, 3), dtype=f)
    wu = np.ascontiguousarray(w_in[:, :, 1024:1280].reshape(L, 8, 128, 2, 128).transpose(0, 2, 3, 1, 4), dtype=f)
    gates = w_in[:, :, 1536:].reshape(L, 8, 128, 3, 8, 128)
    gates = gates.transpose(0, 4, 2, 3, 1, 5).reshape(L, 8, 128, 24, 128)
    proj = np.concatenate([w_proj_a, w_proj_b, w_proj_c], axis=1)
    proj = proj.reshape(L, 8, 128, 8, 128).transpose(0, 3, 2, 1, 4)
    wgp = np.ascontiguousarray(np.concatenate([gates, proj], axis=3), dtype=f)
    wout = np.ascontiguousarray(w_out.reshape(L, 8, 128, 8, 128).transpose(0, 3, 2, 1, 4), dtype=f)
    wup = w_up.reshape(L, 8, 128, 2, NJ, 128).transpose(0, 4, 2, 3, 1, 5)
    wup = np.ascontiguousarray(wup, dtype=f)
    wdn = np.ascontiguousarray(w_down.reshape(L, NJ, 128, 8, 128).transpose(0, 3, 2, 1, 4), dtype=f)
    vecs = np.zeros((L, 128, NV), f)
    vecs[:, :, 0:8] = norm1.reshape(L, 8, 128).transpose(0, 2, 1)
    vecs[:, :, 8:16] = norm2.reshape(L, 8, 128).transpose(0, 2, 1)
    vecs[:, :, 16:18] = pool_scale.reshape(L, 2, 128).transpose(0, 2, 1)
    vecs[:, :, 18:62] = conv_b.reshape(L, 44, 128).transpose(0, 2, 1)
    vecs[:, :, 62:194] = conv_w.reshape(L, 3, 44, 128).transpose(0, 3, 2, 1).reshape(L, 128, 132)
    rows = np.zeros((L, 128, NR), f)
    rows[:, :, 0:64] = q_norm[:, None, :]
    rows[:, :, 64:128] = k_norm[:, None, :]
    rows[:, :, 128:192] = sgu_v_norm[:, None, :]
    rows[:, :, 192:200] = sinks[:, None, :]
    brep = np.ascontiguousarray(b_s.reshape(L, 2, 2, 1, 128).repeat(64, axis=3).transpose(0, 2, 3, 1, 4).reshape(L, 128, 2, 128), dtype=f)
    wsT = np.ascontiguousarray(w_s.transpose(0, 3, 1, 2), dtype=f)
    wpbd = np.zeros((L, 128, 2, 128), f)
    for cc in range(2):
        for h in range(2):
            wpbd[:, h * 64:(h + 1) * 64, cc, h * 64:(h + 1) * 64] = w_pool[:, cc * 2 + h]
    return dict(wtm=wtm, wu=wu, wgp=wgp, wout=wout, wup=wup, wdn=wdn, vecs=vecs, rows=rows, brep=brep, wsT=wsT, wpbd=wpbd)


_NC_CACHE = {}


def kernel(x, positions, norm1, w_in, q_norm, k_norm, sinks, w_pool, pool_scale, sgu_v_norm, w_s, b_s,
           w_proj_a, w_proj_b, w_proj_c, w_out, norm2, w_up, conv_w, conv_b, w_down):
    args = [np.asarray(a) for a in (norm1, w_in, q_norm, k_norm, sinks, w_pool, pool_scale, sgu_v_norm, w_s, b_s,
                                    w_proj_a, w_proj_b, w_proj_c, w_out, norm2, w_up, conv_w, conv_b, w_down)]
    x = np.asarray(x, dtype=np.float32)
    positions = np.asarray(positions).astype(np.int32)
    W = _prep_weights(*args)
    maskc, maskp, poolA, poolA_first, invf = _consts()
    in_maps = []
    for core in range(NCORES):
        b, j = core // 4, core % 4
        t0 = j * OWN - HALO
        xs = np.zeros((WIN, D), np.float32)
        ps_ = np.zeros((WIN,), np.int32)
        lo = max(t0, 0)
        xs[lo - t0:] = x[b, lo:t0 + WIN]
        ps_[lo - t0:] = positions[b, lo:t0 + WIN]
        first = (j == 0)
        m = dict(W)
        m["xw"] = np.ascontiguousarray(xs.T.reshape(8, 128, WIN))
        m["posw"] = np.ascontiguousarray(ps_.reshape(NBLK, 128).T)
        m["hv"] = np.full((128, 1), 0.0 if first else 1.0, np.float32)
        m["mask4"] = np.zeros_like(maskp) if first else maskp
        m["poolA4"] = poolA_first if first else poolA
        m["maskc"] = maskc
        m["maskp"] = maskp
        m["poolA"] = poolA
        m["invf"] = invf
        in_maps.append(m)
    if "nc" not in _NC_CACHE:
        _NC_CACHE["nc"] = build_program()
    res = run_bass_kernel_spmd(_NC_CACHE["nc"], in_maps, core_ids=list(range(NCORES)))
    y = np.empty((2, SEQ, D), np.float32)
    for core in range(NCORES):
        b, j = core // 4, core % 4
        o = res.results[core]["out"]
        y[b, j * OWN:(j + 1) * OWN, :] = o.reshape(D, OWN).T
    return y
```

```python
import math
import numpy as np
from contextlib import ExitStack
import concourse.bass as bass
import concourse.mybir as mybir
from concourse.alu_op_type import AluOpType as ALU
from concourse.bass_utils import run_bass_kernel_spmd

AF = mybir.ActivationFunctionType
F32 = mybir.dt.float32
BF16 = mybir.dt.bfloat16
I32 = mybir.dt.int32
AX = mybir.AxisListType

ENG = ['pe', 'act', 'dve', 'pool', 'sp']
NCORES = 8
D = 1024
SEQ = 8192
OWN = 2048
HALO = 512
WIN = OWN + HALO
NSUB = WIN // 512
NBLK = WIN // 128
DFF = 2816
NJ = DFF // 128
EPS = 1e-6
NV = 8 + 8 + 2 + 44 + 132
NR = 64 * 3 + 8


class Sched:
    def __init__(self, nc, stack, self_sync=('act', 'dve', 'pool')):
        self.nc = nc
        self.stack = stack
        self.engs = {'pe': nc.tensor, 'act': nc.scalar, 'dve': nc.vector,
                     'pool': nc.gpsimd, 'sp': nc.sync}
        self.sem = {e: stack.enter_context(nc.semaphore("s_" + e)) for e in ENG}
        self.cnt = {e: 0 for e in ENG}
        self.known = {e: {} for e in ENG}
        self.snap = {}
        self.lastw = {}
        self.readers = {}
        self.dsem = {}
        self.dcnt = {}
        self.self_sync = set(self_sync)
        self.nwaits = 0
        self.nops = 0
        self.oplog = []

    def _handle(self, key):
        return self.sem[key] if key in self.sem else self.dsem[key]

    def _merge(self, e, key, val):
        k = self.known[e]
        if k.get(key, 0) < val:
            k[key] = val
        s = self.snap.get((key, val))
        if s:
            for kk, vv in s.items():
                if k.get(kk, 0) < vv:
                    k[kk] = vv

    def _deps(self, e, reads, writes):
        need = {}

        def add(kv):
            if kv is None:
                return
            key, val = kv
            if key == e and e not in self.self_sync:
                return
            if need.get(key, 0) < val:
                need[key] = val
        for r in reads:
            add(self.lastw.get(r))
        for r in writes:
            add(self.lastw.get(r))
            for kv in self.readers.get(r, {}).items():
                add(kv)
        eng = self.engs[e]
        for key, val in need.items():
            if self.known[e].get(key, 0) >= val:
                continue
            eng.wait_ge(self._handle(key), val)
            self.nwaits += 1
            self._merge(e, key, val)

    def _record(self, key, val, reads, writes):
        for r in reads:
            self.readers.setdefault(r, {})[key] = val
        for r in writes:
            self.lastw[r] = (key, val)
            self.readers[r] = {}

    limit = None

    def op(self, e, emit, reads=(), writes=()):
        if self.limit is not None and self.nops >= self.limit:
            return None
        self._deps(e, reads, writes)
        import sys as _sys
        self.oplog.append((self.nops, e, _sys._getframe(2).f_lineno, list(writes)))
        inst = emit(self.engs[e])
        self.cnt[e] += 1
        val = self.cnt[e]
        inst.then_inc(self.sem[e], 1)
        if e not in self.self_sync:
            self.known[e][e] = val
        self.snap[(e, val)] = dict(self.known[e])
        self._record(e, val, reads, writes)
        self.nops += 1
        return inst

    def dma(self, q, semname, out, in_, reads=(), writes=()):
        if self.limit is not None and self.nops >= self.limit:
            return None
        if semname not in self.dsem:
            self.dsem[semname] = self.stack.enter_context(self.nc.semaphore("d_" + semname))
            self.dcnt[semname] = 0
        self._deps(q, reads, writes)
        inst = self.engs[q].dma_start(out=out, in_=in_)
        self.dcnt[semname] += 16
        val = self.dcnt[semname]
        inst.then_inc(self.dsem[semname], 16)
        self.snap[(semname, val)] = dict(self.known[q])
        self._record(semname, val, reads, writes)
        return inst

    def wait_all(self, e):
        eng = self.engs[e]
        for k in ENG:
            if k != e and self.cnt[k] > 0:
                eng.wait_ge(self.sem[k], self.cnt[k])
        for k, v in self.dcnt.items():
            eng.wait_ge(self.dsem[k], v)


class Stream:
    def __init__(self, S, name, slots, seq):
        self.S = S
        self.name = name
        self.slots = slots
        self.seq = seq
        self.nload = 0
        self.nuse = 0
        for _ in range(len(slots)):
            self.load_next()

    def res(self, k):
        return "%s_%d" % (self.name, k)

    def load_next(self):
        if self.nload >= len(self.seq):
            return
        k = self.nload % len(self.slots)
        self.S.dma('pool', self.res(k), self.slots[k], self.seq[self.nload], writes=[self.res(k)])
        self.nload += 1

    def cur(self):
        k = self.nuse % len(self.slots)
        return self.slots[k], self.res(k)

    def done(self):
        self.nuse += 1
        self.load_next()


def build_program(nsub=NSUB, store_all=False, limit=None):
    nc = bass.Bass("TRN2", target_bir_lowering=False)

    def din(name, shape, dt=F32):
        return nc.dram_tensor(name, list(shape), dt, kind="ExternalInput").ap()

    xw = din("xw", [8, 128, WIN])
    posw = din("posw", [128, NBLK], I32)
    hv_d = din("hv", [128, 1])
    mask4_d = din("mask4", [128, 128])
    poolA4_d = din("poolA4", [128, 8, 128])
    maskc_d = din("maskc", [128, 128])
    maskp_d = din("maskp", [128, 128])
    poolA_d = din("poolA", [128, 8, 128])
    invf_d = din("invf", [128, 32])
    vecs_d = din("vecs", [2, 128, NV])
    rows_d = din("rows", [2, 128, NR])
    brep_d = din("brep", [2, 128, 2, 128])
    wsT_d = din("wsT", [2, 128, 4, 128])
    wpbd_d = din("wpbd", [2, 128, 2, 128])
    wtm_d = din("wtm", [2, 128, 8, 1280])
    wu_d = din("wu", [2, 128, 2, 8, 128])
    wgp_d = din("wgp", [2, 8, 128, 32, 128])
    wout_d = din("wout", [2, 8, 128, 8, 128])
    wup_d = din("wup", [2, NJ, 128, 2, 8, 128])
    wdn_d = din("wdn", [2, 8, 128, NJ, 128])
    out_d = nc.dram_tensor("out", [8, 128, OWN], F32, kind="ExternalOutput").ap()

    with ExitStack() as st:
        S = Sched(nc, st)
        S.limit = limit

        def sb(name, shape, dt):
            return st.enter_context(nc.sbuf_tensor(name, list(shape), dt))

        X = sb("X", [128, 2, 8, 512], F32)
        big = sb("big", [128, 22, 512], BF16)
        hT = sb("hT", [128, 8, 512], BF16)
        rs = sb("rs", [128, 512], F32)
        NF = 10
        Fr = sb("Fr", [128, NF, 640], F32)
        kT = sb("kT", [128, 2, 640], BF16)
        qT = sb("qT", [128, 4, 512], BF16)
        Vaug = sb("Vaug", [128, 2, 5, 2, 66], BF16)
        xab = sb("xab", [128, 2, 5, 256], BF16)
        vgb = sb("vgb", [128, 2, 256], BF16)
        qkn = sb("qkn", [128, 2, 640], BF16)
        ex = sb("ex", [128, 4, 512], BF16)
        PT = sb("PT", [128, 4, 512], BF16)
        attn_tm = sb("attn_tm", [128, 2, 512], BF16)
        sm = sb("sm", [128, 2, 48], F32)
        uT = sb("uT", [128, 2, 512], F32)
        diffT = sb("diffT", [128, 2, 512], BF16)
        aT = sb("aT", [128, 2, 512], BF16)
        tails = sb("tails", [128, 2, 44, 2], F32)
        ident = sb("ident", [128, 128], BF16)
        ones = sb("ones", [128, 128], BF16)
        maskc = sb("maskc_s", [128, 128], BF16)
        maskp = sb("maskp_s", [128, 128], BF16)
        mask4 = sb("mask4_s", [128, 128], BF16)
        poolA = sb("poolA_s", [128, 8, 128], BF16)
        poolA4 = sb("poolA4_s", [128, 8, 128], BF16)
        wsT = sb("wsT_s", [128, 2, 4, 128], BF16)
        wpbd = sb("wpbd_s", [128, 2, 2, 128], BF16)
        brep = sb("brep_s", [128, 2, 2, 128], F32)
        vecs = sb("vecs_s", [128, 2, NV], F32)
        rows = sb("rows_s", [128, 2, NR], F32)
        esink = sb("esink", [128, 2, 8], F32)
        hv = sb("hv_s", [128, 1], F32)
        cst = sb("cst", [128, 4], F32)
        invf = sb("invf_s", [128, 32], F32)
        posi = sb("posi", [128, NBLK], I32)
        posf = sb("posf", [128, NBLK], F32)
        cosT = sb("cosT", [128, NBLK, 32], F32)
        sinT = sb("sinT", [128, NBLK, 32], F32)
        Wtm = sb("Wtm", [128, 8, 1280], BF16)
        Wu = sb("Wu", [128, 2, 8, 128], BF16)
        Wgp = sb("Wgp", [128, 2, 32, 128], BF16)
        Wout = sb("Wout", [128, 2, 8, 128], BF16)
        Wup = sb("Wup", [128, 2, 2, 8, 128], BF16)
        Wdn = sb("Wdn", [128, 2, NJ, 128], BF16)
        ps = st.enter_context(nc.psum_tensor("ps", [128, 8, 512], F32))

        state = {'bank': 0, 'f': 0}

        def bank():
            b = state['bank']
            state['bank'] = (b + 1) % 8
            return b, "ps%d" % b

        def fslot():
            f = state['f']
            state['f'] = (f + 1) % NF
            return Fr[:, f, :], "F%d" % f

        def mm_group(out_ap, pairs, reads, writes):
            n = len(pairs)

            def emit(e):
                inst = None
                for i, (l, r) in enumerate(pairs):
                    inst = e.matmul(out_ap, lhsT=l, rhs=r, start=(i == 0), stop=(i == n - 1))
                return inst
            S.op('pe', emit, reads=reads, writes=writes)

        def OP(e, fn, reads, writes):
            S.op(e, fn, reads=reads, writes=writes)

        order = [(s, l) for s in range(nsub) for l in range(2)]
        st_wtm = Stream(S, "Wtm", [Wtm[:]], [wtm_d[l] for (s, l) in order])
        st_wu = Stream(S, "Wu", [Wu[:]], [wu_d[l] for (s, l) in order])
        st_wgp = Stream(S, "Wgp", [Wgp[:, k] for k in range(2)], [wgp_d[l, c] for (s, l) in order for c in range(8)])
        st_wout = Stream(S, "Wout", [Wout[:, k] for k in range(2)], [wout_d[l, c] for (s, l) in order for c in range(8)])
        st_wup = Stream(S, "Wup", [Wup[:, k] for k in range(2)], [wup_d[l, j] for (s, l) in order for j in range(NJ)])
        st_wdn = Stream(S, "Wdn", [Wdn[:, k] for k in range(2)], [wdn_d[l, c] for (s, l) in order for c in range(8)])

        S.dma('pool', 'c_maskc', maskc[:], maskc_d, writes=['maskc'])
        S.dma('pool', 'c_maskp', maskp[:], maskp_d, writes=['maskp'])
        S.dma('pool', 'c_mask4', mask4[:], mask4_d, writes=['mask4'])
        S.dma('pool', 'c_poolA', poolA[:], poolA_d, writes=['poolA'])
        S.dma('pool', 'c_poolA4', poolA4[:], poolA4_d, writes=['poolA4'])
        for l in range(2):
            S.dma('pool', 'c_wsT%d' % l, wsT[:, l], wsT_d[l], writes=['wsT%d' % l])
            S.dma('pool', 'c_wpbd%d' % l, wpbd[:, l], wpbd_d[l], writes=['wpbd%d' % l])
            S.dma('sp', 'c_brep%d' % l, brep[:, l], brep_d[l], writes=['brep%d' % l])
            S.dma('sp', 'c_vecs%d' % l, vecs[:, l], vecs_d[l], writes=['vecs'])
            S.dma('sp', 'c_rows%d' % l, rows[:, l], rows_d[l], writes=['rows'])
        S.dma('sp', 'c_hv', hv[:], hv_d, writes=['hv'])
        S.dma('sp', 'c_invf', invf[:], invf_d, writes=['invf'])
        S.dma('sp', 'c_pos', posi[:], posw, writes=['posi'])

        OP('pool', lambda e: e.memset(ident[:], 0.0), [], ['ident'])
        OP('pool', lambda e: e.affine_select(out=ident[:], in_=ident[:], pattern=[[-1, 128]],
                                             compare_op=ALU.not_equal, fill=1.0, base=0,
                                             channel_multiplier=1), ['ident'], ['ident'])
        OP('dve', lambda e: e.memset(ones[:], 1.0), [], ['ones'])
        OP('dve', lambda e: e.memset(cst[:, 0:1], -0.5), [], ['cst'])
        OP('dve', lambda e: e.memset(cst[:, 1:2], EPS), ['cst'], ['cst'])
        OP('dve', lambda e: e.memset(tails[:], 0.0), [], ['tails0', 'tails1'])
        OP('dve', lambda e: e.memset(kT[:], 0.0), [], ['kT0', 'kT1'])
        OP('dve', lambda e: e.memset(Vaug[:], 1.0), [], ['Vaug0', 'Vaug1'])
        OP('dve', lambda e: e.memset(xab[:], 0.0), [], ['xab0', 'xab1'])
        for l in range(2):
            OP('dve', lambda e, l=l: e.tensor_tensor(out=wsT[:, l], in0=wsT[:, l],
                                                     in1=maskc[:].unsqueeze(1).broadcast_to([128, 4, 128]), op=ALU.mult),
               ['wsT%d' % l, 'maskc'], ['wsT%d' % l])
            OP('act', lambda e, l=l: e.activation(out=esink[:, l, :], in_=rows[:, l, 192:200], func=AF.Exp),
               ['rows'], ['esink'])

        OP('dve', lambda e: e.tensor_copy(out=posf[:], in_=posi[:]), ['posi'], ['posf'])
        ang, angr = fslot()
        ang = ang[:, 0:NBLK * 32].rearrange("p (a b) -> p a b", a=NBLK)
        OP('dve', lambda e: e.tensor_tensor(out=ang, in0=posf[:].unsqueeze(2).broadcast_to([128, NBLK, 32]),
                                            in1=invf[:].unsqueeze(1).broadcast_to([128, NBLK, 32]), op=ALU.mult),
           ['posf', 'invf'], [angr])
        MAGIC = 12582912.0
        TWO_PI = 2.0 * math.pi
        C1 = 6.28125
        C2 = float(np.float32(TWO_PI - C1))
        C3 = float(TWO_PI - C1 - C2)
        tk, tkr = fslot()
        tk = tk[:, 0:NBLK * 32].rearrange("p (a b) -> p a b", a=NBLK)
        rr, rrr = fslot()
        rr = rr[:, 0:NBLK * 32].rearrange("p (a b) -> p a b", a=NBLK)
        r2, r2r = fslot()
        r2 = r2[:, 0:NBLK * 32].rearrange("p (a b) -> p a b", a=NBLK)
        OP('dve', lambda e: e.tensor_scalar(out=tk, in0=ang, scalar1=1.0 / TWO_PI, scalar2=MAGIC, op0=ALU.mult, op1=ALU.add), [angr], [tkr])
        OP('dve', lambda e: e.tensor_scalar(out=tk, in0=tk, scalar1=-MAGIC, scalar2=None, op0=ALU.add), [tkr], [tkr])
        OP('dve', lambda e: e.scalar_tensor_tensor(out=rr, in0=tk, scalar=-C1, in1=ang, op0=ALU.mult, op1=ALU.add), [tkr, angr], [rrr])
        OP('dve', lambda e: e.scalar_tensor_tensor(out=rr, in0=tk, scalar=-C2, in1=rr, op0=ALU.mult, op1=ALU.add), [tkr, rrr], [rrr])
        OP('dve', lambda e: e.scalar_tensor_tensor(out=rr, in0=tk, scalar=-C3, in1=rr, op0=ALU.mult, op1=ALU.add), [tkr, rrr], [rrr])
        OP('dve', lambda e: e.tensor_scalar(out=r2, in0=rr, scalar1=math.pi / 2, scalar2=None, op0=ALU.add), [rrr], [r2r])
        OP('dve', lambda e: e.tensor_scalar(out=tk, in0=r2, scalar1=math.pi, scalar2=-TWO_PI, op0=ALU.is_gt, op1=ALU.mult), [r2r], [tkr])
        OP('dve', lambda e: e.tensor_tensor(out=r2, in0=r2, in1=tk, op=ALU.add), [r2r, tkr], [r2r])
        PI_SAFE = 3.1415925
        OP('dve', lambda e: e.tensor_scalar(out=rr, in0=rr, scalar1=PI_SAFE, scalar2=-PI_SAFE, op0=ALU.min, op1=ALU.max), [rrr], [rrr])
        OP('dve', lambda e: e.tensor_scalar(out=r2, in0=r2, scalar1=PI_SAFE, scalar2=-PI_SAFE, op0=ALU.min, op1=ALU.max), [r2r], [r2r])
        OP('act', lambda e: e.activation(out=sinT[:], in_=rr, func=AF.Sin), [rrr], ['sinT'])
        OP('act', lambda e: e.activation(out=cosT[:], in_=r2, func=AF.Sin), [r2r], ['cosT'])

        def load_x(s):
            xb = s % 2
            for c in range(8):
                S.dma('sp', 'x%d_%d' % (xb, c), X[:, xb, c, :], xw[c, :, s * 512:(s + 1) * 512],
                      writes=['X%d_%d' % (xb, c)])

        def rmsnorm(xb, l, gcol):
            for c in range(8):
                OP('act', lambda e, c=c: e.activation(out=big[:, 8 + c, :], in_=X[:, xb, c, :], func=AF.Square),
                   ['X%d_%d' % (xb, c)], ['big%d' % (8 + c)])
            b, br = bank()
            mm_group(ps[:, b, :], [(ones[:], big[:, 8 + c, :]) for c in range(8)],
                     ['ones'] + ['big%d' % (8 + c) for c in range(8)], [br])
            OP('act', lambda e: e.activation(out=rs[:], in_=ps[:, b, :], func=AF.Sqrt, scale=1.0 / D, bias=cst[:, 1:2]),
               [br, 'cst'], ['rs'])
            OP('dve', lambda e: e.reciprocal(out=rs[:], in_=rs[:]), ['rs'], ['rs'])
            for c in range(8):
                OP('dve', lambda e, c=c: e.scalar_tensor_tensor(out=hT[:, c, :], in0=X[:, xb, c, :], scalar=vecs[:, l, gcol + c:gcol + c + 1],
                                                               in1=rs[:], op0=ALU.mult, op1=ALU.mult),
                   ['X%d_%d' % (xb, c), 'rs', 'vecs'], ['hT%d' % c])

        HT = ['hT%d' % c for c in range(8)]

        def gelu2(src_ap, src_res, n, dst_ap, dst_res):
            t1, t1r = fslot()
            t1 = t1[:, 0:n]
            OP('act', lambda e: e.activation(out=t1, in_=src_ap, func=AF.Square), [src_res], [t1r])
            OP('dve', lambda e: e.tensor_scalar(out=t1, in0=t1, scalar1=0.044715, scalar2=1.0, op0=ALU.mult, op1=ALU.add), [t1r], [t1r])
            OP('dve', lambda e: e.tensor_tensor(out=t1, in0=t1, in1=src_ap, op=ALU.mult), [t1r, src_res], [t1r])
            OP('act', lambda e: e.activation(out=t1, in_=t1, func=AF.Tanh, scale=0.7978845608028654), [t1r], [t1r])
            OP('dve', lambda e: e.scalar_tensor_tensor(out=dst_ap, in0=t1, scalar=1.0, in1=src_ap, op0=ALU.add, op1=ALU.mult),
               [t1r, src_res], [dst_res])

        def mixer(s, l):
            xb = s % 2
            rmsnorm(xb, l, 0)
            w, wr = st_wu.cur()
            for cc in range(2):
                b, br = bank()
                mm_group(ps[:, b, :], [(w[:, cc, kc, :], hT[:, kc, :]) for kc in range(8)], [wr] + HT, [br])
                gelu2(ps[:, b, :], br, 512, uT[:, cc, :], 'uT%d' % cc)
            st_wu.done()
            wtm, wtmr = st_wtm.cur()
            KT = 'kT%d' % l
            OP('dve', lambda e: e.tensor_copy(out=kT[:, l, 0:128], in_=kT[:, l, 512:640]), [KT], [KT])
            OP('dve', lambda e: e.tensor_copy(out=Vaug[:, l, 0], in_=Vaug[:, l, 4]), ['Vaug%d' % l], ['Vaug%d' % l])
            OP('dve', lambda e: e.tensor_copy(out=xab[:, l, 0], in_=xab[:, l, 4]), ['xab%d' % l], ['xab%d' % l])
            for bi in range(4):
                blk = s * 4 + bi
                tb = bi * 128
                par = bi % 2
                bA, bAr = bank()
                bB, bBr = bank()
                bC, bCr = bank()
                mm_group(ps[:, bA, :], [(hT[:, kc, tb:tb + 128], wtm[:, kc, 0:512]) for kc in range(8)], [wtmr] + HT, [bAr])
                mm_group(ps[:, bB, :], [(hT[:, kc, tb:tb + 128], wtm[:, kc, 512:1024]) for kc in range(8)], [wtmr] + HT, [bBr])
                mm_group(ps[:, bC, 0:256], [(hT[:, kc, tb:tb + 128], wtm[:, kc, 1024:1280]) for kc in range(8)], [wtmr] + HT, [bCr])
                if bi == 3:
                    st_wtm.done()
                SM = 'sm%d' % par
                ss = sm[:, par, 0:14]
                rq = sm[:, par, 16:30]
                t1, t1r = fslot()
                OP('act', lambda e: e.activation(out=t1[:, 0:512], in_=ps[:, bA, :], func=AF.Square), [bAr], [t1r])
                OP('act', lambda e: e.activation(out=t1[:, 512:640], in_=ps[:, bB, 0:128], func=AF.Square), [bBr, t1r], [t1r])
                OP('dve', lambda e: e.tensor_reduce(out=ss[:, 0:10], in_=t1[:, 0:640].rearrange("p (a b) -> p a b", a=10), axis=AX.X, op=ALU.add),
                   [t1r], [SM])
                gl, glr = fslot()
                gelu2(ps[:, bB, 256:512], bBr, 256, gl[:, 0:256], glr)
                OP('act', lambda e: e.activation(out=gl[:, 256:512], in_=gl[:, 0:256], func=AF.Square, scale=0.5), [glr], [glr])
                OP('dve', lambda e: e.tensor_reduce(out=ss[:, 10:14], in_=gl[:, 256:512].rearrange("p (a b) -> p a b", a=4), axis=AX.X, op=ALU.add),
                   [glr, SM], [SM])
                OP('dve', lambda e: e.tensor_scalar(out=rq, in0=ss, scalar1=1.0 / 64, scalar2=EPS, op0=ALU.mult, op1=ALU.add), [SM], [SM])
                OP('pool', lambda e: e.tensor_tensor(out=rq, in0=rq, in1=cst[:, 0:1].broadcast_to([128, 14]), op=ALU.pow), [SM, 'cst'], [SM])
                qkg, qkgr = fslot()
                OP('dve', lambda e: e.tensor_tensor(out=qkg[:, 0:512].rearrange("p (a b) -> p a b", a=8),
                                                    in0=ps[:, bA, :].rearrange("p (a b) -> p a b", a=8),
                                                    in1=rows[:, l, 0:64].unsqueeze(1).broadcast_to([128, 8, 64]), op=ALU.mult),
                   [bAr, 'rows'], [qkgr])
                OP('dve', lambda e: e.tensor_tensor(out=qkg[:, 512:640].rearrange("p (a b) -> p a b", a=2),
                                                    in0=ps[:, bB, 0:128].rearrange("p (a b) -> p a b", a=2),
                                                    in1=rows[:, l, 64:128].unsqueeze(1).broadcast_to([128, 2, 64]), op=ALU.mult),
                   [bBr, 'rows', qkgr], [qkgr])
                H = qkg[:, 0:640].rearrange("p (a h b) -> p a h b", a=10, h=2)
                t1v, t2v = H[:, :, 0, :], H[:, :, 1, :]
                cb_ = cosT[:, blk, :].unsqueeze(1).broadcast_to([128, 10, 32])
                sb_ = sinT[:, blk, :].unsqueeze(1).broadcast_to([128, 10, 32])
                ta, tar = fslot()
                A_ = ta[:, 0:320].rearrange("p (a b) -> p a b", a=10)
                B_ = ta[:, 320:640].rearrange("p (a b) -> p a b", a=10)
                tc_, tcr = fslot()
                C_ = tc_[:, 0:320].rearrange("p (a b) -> p a b", a=10)
                D_ = tc_[:, 320:640].rearrange("p (a b) -> p a b", a=10)
                ro, ror = fslot()
                RO = ro[:, 0:640].rearrange("p (a h b) -> p a h b", a=10, h=2)
                OP('dve', lambda e: e.tensor_tensor(out=A_, in0=t1v, in1=cb_, op=ALU.mult), [qkgr, 'cosT'], [tar])
                OP('dve', lambda e: e.tensor_tensor(out=B_, in0=t2v, in1=sb_, op=ALU.mult), [qkgr, 'sinT', tar], [tar])
                OP('dve', lambda e: e.tensor_tensor(out=C_, in0=t2v, in1=cb_, op=ALU.mult), [qkgr, 'cosT'], [tcr])
                OP('dve', lambda e: e.tensor_tensor(out=D_, in0=t1v, in1=sb_, op=ALU.mult), [qkgr, 'sinT', tcr], [tcr])
                OP('dve', lambda e: e.tensor_tensor(out=RO[:, :, 0, :], in0=A_, in1=B_, op=ALU.subtract), [tar], [ror])
                OP('dve', lambda e: e.tensor_tensor(out=RO[:, :, 1, :], in0=C_, in1=D_, op=ALU.add), [tcr, ror], [ror])
                QKN = 'qkn%d' % par
                OP('dve', lambda e: e.tensor_tensor(out=qkn[:, par, :].rearrange("p (a b) -> p a b", a=10),
                                                    in0=ro[:, 0:640].rearrange("p (a b) -> p a b", a=10),
                                                    in1=rq[:, 0:10].unsqueeze(2).broadcast_to([128, 10, 64]), op=ALU.mult),
                   [ror, SM], [QKN])
                VA = 'Vaug%d' % l
                OP('act', lambda e: e.copy(out=Vaug[:, l, bi + 1, :, 0:64], in_=ps[:, bB, 128:256].rearrange("p (a b) -> p a b", a=2)),
                   [bBr], [VA])
                OP('dve', lambda e: e.scalar_tensor_tensor(out=gl[:, 256:512].rearrange("p (a b) -> p a b", a=4),
                                                           in0=gl[:, 0:256].rearrange("p (a b) -> p a b", a=4), scalar=0.5,
                                                           in1=rows[:, l, 128:192].unsqueeze(1).broadcast_to([128, 4, 64]),
                                                           op0=ALU.mult, op1=ALU.mult), [glr, 'rows'], [glr])
                VG = 'vgb%d' % par
                OP('dve', lambda e: e.tensor_tensor(out=vgb[:, par, :].rearrange("p (a b) -> p a b", a=4),
                                                    in0=gl[:, 256:512].rearrange("p (a b) -> p a b", a=4),
                                                    in1=rq[:, 10:14].unsqueeze(2).broadcast_to([128, 4, 64]), op=ALU.mult),
                   [glr, SM], [VG])
                XA = 'xab%d' % l
                OP('act', lambda e: e.copy(out=xab[:, l, bi + 1, :], in_=ps[:, bC, 0:256]), [bCr], [XA])
                bT, bTr = bank()
                psb = ps[:, bT, :].bitcast(BF16)

                def emit_tr(e):
                    inst = None
                    for i in range(5):
                        inst = e.transpose(out=psb[:, i * 128:(i + 1) * 128], in_=qkn[:, par, i * 128:(i + 1) * 128], identity=ident[:])
                    return inst
                OP('pe', emit_tr, [QKN, 'ident'], [bTr])
                OP('act', lambda e: e.copy(out=qT[:, :, tb:tb + 128], in_=psb[:, 0:512].rearrange("p (a b) -> p a b", a=4)), [bTr], ['qT'])
                OP('act', lambda e: e.copy(out=kT[:, l, 128 + tb:256 + tb], in_=psb[:, 512:640]), [bTr], [KT])
                pvsrc = []
                for kb in range(2):
                    koff = tb + kb * 128
                    if kb == 0:
                        mk, mkr = (mask4, 'mask4') if blk == 4 else (maskp, 'maskp')
                    else:
                        mk, mkr = maskc, 'maskc'
                    for grp in range(2):
                        idx = kb * 2 + grp
                        bS, bSr = bank()
                        OP('pe', lambda e, bS=bS, grp=grp, koff=koff: e.matmul(
                            ps[:, bS, :].rearrange("p (a b) -> p a b", a=4),
                            lhsT=kT[grp * 64:(grp + 1) * 64, l, koff:koff + 128],
                            rhs=qT[grp * 64:(grp + 1) * 64, :, tb:tb + 128], start=True, stop=True),
                           [KT, 'qT'], [bSr])
                        OP('act', lambda e, bS=bS, idx=idx: e.activation(out=ex[:, idx, :], in_=ps[:, bS, :], func=AF.Exp, scale=0.125),
                           [bSr], ['ex%d' % idx])
                        OP('dve', lambda e, idx=idx, mk=mk: e.tensor_tensor(
                            out=PT[:, idx, :].rearrange("p (a b) -> p a b", a=4),
                            in0=ex[:, idx, :].rearrange("p (a b) -> p a b", a=4),
                            in1=mk[:].unsqueeze(1).broadcast_to([128, 4, 128]), op=ALU.mult),
                           ['ex%d' % idx, mkr], ['PT%d' % idx])
                obanks = []
                for grp in range(2):
                    bO, bOr = bank()
                    obanks.append((bO, bOr))

                    def emit_pv(e, grp=grp, bO=bO):
                        inst = None
                        for c in range(4):
                            for kb in range(2):
                                inst = e.matmul(ps[:, bO, c * 128:c * 128 + 65],
                                                lhsT=PT[:, kb * 2 + grp, c * 128:(c + 1) * 128],
                                                rhs=Vaug[:, l, bi + kb, grp, 0:65], start=(kb == 0), stop=(kb == 1))
                        return inst
                    OP('pe', emit_pv, ['PT%d' % grp, 'PT%d' % (2 + grp), VA], [bOr])
                den = sm[:, par, 32:40]
                for grp in range(2):
                    bO, bOr = obanks[grp]
                    OP('dve', lambda e, grp=grp, bO=bO: e.tensor_tensor(
                        out=den[:, grp * 4:(grp + 1) * 4],
                        in0=ps[:, bO, :].rearrange("p (a b) -> p a b", a=4)[:, :, 64],
                        in1=esink[:, l, grp * 4:(grp + 1) * 4], op=ALU.add), [bOr, 'esink', SM], [SM])
                OP('dve', lambda e: e.reciprocal(out=den, in_=den), [SM], [SM])
                AT = 'attn_tm%d' % par
                for grp in range(2):
                    bO, bOr = obanks[grp]
                    OP('dve', lambda e, grp=grp, bO=bO: e.tensor_tensor(
                        out=attn_tm[:, par, grp * 256:(grp + 1) * 256].rearrange("p (a b) -> p a b", a=4),
                        in0=ps[:, bO, :].rearrange("p (a b) -> p a b", a=4)[:, :, 0:64],
                        in1=den[:, grp * 4:(grp + 1) * 4].unsqueeze(2).broadcast_to([128, 4, 64]), op=ALU.mult),
                       [bOr, SM], [AT])
                bT2, bT2r = bank()
                psb2 = ps[:, bT2, :].bitcast(BF16)

                def emit_tr2(e):
                    inst = None
                    for i in range(4):
                        inst = e.transpose(out=psb2[:, i * 128:(i + 1) * 128], in_=attn_tm[:, par, i * 128:(i + 1) * 128], identity=ident[:])
                    return inst
                OP('pe', emit_tr2, [AT, 'ident'], [bT2r])
                OP('act', lambda e: e.copy(out=big[:, 16:20, tb:tb + 128], in_=psb2[:, 0:512].rearrange("p (a b) -> p a b", a=4)),
                   [bT2r], ['big16', 'big17', 'big18', 'big19'])
                bG, bGr = bank()

                def emit_sgu(e):
                    inst = None
                    for g in range(4):
                        cc = g // 2
                        inst = e.matmul(ps[:, bG, g * 128:(g + 1) * 128], lhsT=vgb[:, par, cc * 128:(cc + 1) * 128],
                                        rhs=wsT[:, l, g, :], start=True, stop=True)
                    return inst
                OP('pe', emit_sgu, [VG, 'wsT%d' % l], [bGr])
                for g in range(4):
                    cc, h = g // 2, g % 2
                    pr = slice(h * 64, (h + 1) * 64)
                    tt, ttr = fslot()
                    OP('dve', lambda e, g=g, cc=cc, pr=pr, tt=tt: e.tensor_tensor(out=tt[pr, 0:128], in0=ps[pr, bG, g * 128:(g + 1) * 128],
                                                                              in1=brep[pr, l, cc, :], op=ALU.add),
                       [bGr, 'brep%d' % l], [ttr])
                    OP('dve', lambda e, cc=cc, pr=pr, tt=tt: e.scalar_tensor_tensor(out=big[pr, 20 + cc, tb:tb + 128], in0=tt[pr, 0:128], scalar=0.5,
                                                                                   in1=uT[pr, cc, tb:tb + 128], op0=ALU.mult, op1=ALU.mult),
                       [ttr, 'uT%d' % cc], ['big%d' % (20 + cc)])
                bP, bPr = bank()
                pA = poolA4 if blk == 4 else poolA
                pAr = 'poolA4' if blk == 4 else 'poolA'

                def emit_pool(e):
                    inst = None
                    for g in range(4):
                        cc = g // 2
                        inst = e.matmul(ps[:, bP, g * 128:(g + 1) * 128], lhsT=xab[:, l, bi + 1, cc * 128:(cc + 1) * 128],
                                        rhs=pA[:, g * 2, :], start=True, stop=False)
                        inst = e.matmul(ps[:, bP, g * 128:(g + 1) * 128], lhsT=xab[:, l, bi, cc * 128:(cc + 1) * 128],
                                        rhs=pA[:, g * 2 + 1, :], start=False, stop=True)
                    return inst
                OP('pe', emit_pool, [XA, pAr], [bPr])
                for g in range(4):
                    cc, h = g // 2, g % 2
                    pr = slice(h * 64, (h + 1) * 64)
                    OP('act', lambda e, g=g, cc=cc, pr=pr: e.copy(out=diffT[pr, cc, tb:tb + 128], in_=ps[pr, bP, g * 128:(g + 1) * 128]),
                       [bPr], ['diffT%d' % cc])
            for cc in range(2):
                b, br = bank()
                mm_group(ps[:, b, :], [(wpbd[:, l, cc, :], diffT[:, cc, :])], ['wpbd%d' % l, 'diffT%d' % cc], [br])
                OP('act', lambda e, cc=cc, b=b: e.activation(out=aT[:, cc, :], in_=ps[:, b, :], func=AF.Identity, scale=vecs[:, l, 16 + cc:17 + cc]),
                   [br, 'vecs'], ['aT%d' % cc])
            srcs = [(aT, ['aT0', 'aT1'], 24, 2, None), (big, ['big16', 'big17', 'big18', 'big19'], 26, 4, 16), (big, ['big20', 'big21'], 30, 2, 20)]
            for c in range(8):
                w, wr = st_wgp.cur()
                gb = []
                for i in range(3):
                    b, br = bank()
                    mm_group(ps[:, b, :], [(w[:, i * 8 + kc, :], hT[:, kc, :]) for kc in range(8)], [wr] + HT, [br])
                    gb.append((b, br))
                yb = []
                for (buf, rres, w0, nk, off) in srcs:
                    b, br = bank()
                    if off is None:
                        pairs = [(w[:, w0 + kk, :], buf[:, kk, :]) for kk in range(nk)]
                    else:
                        pairs = [(w[:, w0 + kk, :], buf[:, off + kk, :]) for kk in range(nk)]
                    mm_group(ps[:, b, :], pairs, [wr] + rres, [br])
                    yb.append((b, br))
                st_wgp.done()
                tts = []
                for i in range(3):
                    th, thr = fslot()
                    OP('act', lambda e, i=i, th=th: e.activation(out=th[:, 0:512], in_=ps[:, gb[i][0], :], func=AF.Tanh, scale=0.5), [gb[i][1]], [thr])
                    OP('dve', lambda e, i=i, th=th: e.scalar_tensor_tensor(out=th[:, 0:512], in0=th[:, 0:512], scalar=1.0, in1=ps[:, yb[i][0], :],
                                                                           op0=ALU.add, op1=ALU.mult), [thr, yb[i][1]], [thr])
                    tts.append((th, thr))
                OP('dve', lambda e: e.tensor_tensor(out=tts[0][0][:, 0:512], in0=tts[0][0][:, 0:512], in1=tts[1][0][:, 0:512], op=ALU.add),
                   [tts[0][1], tts[1][1]], [tts[0][1]])
                OP('dve', lambda e, c=c: e.tensor_tensor(out=big[:, 8 + c, :], in0=tts[0][0][:, 0:512], in1=tts[2][0][:, 0:512], op=ALU.add),
                   [tts[0][1], tts[2][1]], ['big%d' % (8 + c)])
            for c in range(8):
                w, wr = st_wout.cur()
                b, br = bank()
                mm_group(ps[:, b, :], [(w[:, kc, :], big[:, 8 + kc, :]) for kc in range(8)], [wr] + ['big%d' % (8 + kc) for kc in range(8)], [br])
                st_wout.done()
                OP('dve', lambda e, c=c, b=b: e.scalar_tensor_tensor(out=X[:, xb, c, :], in0=ps[:, b, :], scalar=0.5, in1=X[:, xb, c, :],
                                                                     op0=ALU.mult, op1=ALU.add), [br, 'X%d_%d' % (xb, c)], ['X%d_%d' % (xb, c)])

        def ffn(s, l, store):
            xb = s % 2
            rmsnorm(xb, l, 8)
            TL = 'tails%d' % l
            for j in range(NJ):
                w, wr = st_wup.cur()
                accs = []
                pbs = []
                for gv in range(2):
                    b, br = bank()
                    mm_group(ps[:, b, :], [(w[:, gv, kc, :], hT[:, kc, :]) for kc in range(8)], [wr] + HT, [br])
                    pbs.append((b, br))
                st_wup.done()
                for gv in range(2):
                    b, br = pbs[gv]
                    ch = gv * NJ + j
                    ub, ubr = fslot()
                    acc, accr = fslot()
                    OP('pool', lambda e, ub=ub, ch=ch: e.tensor_copy(out=ub[:, 0:2], in_=tails[:, l, ch, :]), [TL], [ubr])
                    OP('act', lambda e, ub=ub, b=b: e.copy(out=ub[:, 2:514], in_=ps[:, b, :]), [br, ubr], [ubr])
                    OP('act', lambda e, acc=acc, b=b, ch=ch: e.activation(out=acc[:, 0:512], in_=ps[:, b, :], func=AF.Identity,
                                                                        scale=vecs[:, l, 62 + ch * 3 + 2:62 + ch * 3 + 3],
                                                                        bias=vecs[:, l, 18 + ch:19 + ch]), [br, 'vecs'], [accr])
                    OP('dve', lambda e, acc=acc, ub=ub, ch=ch: e.scalar_tensor_tensor(out=acc[:, 0:512], in0=ub[:, 1:513],
                                                                                     scalar=vecs[:, l, 62 + ch * 3 + 1:62 + ch * 3 + 2],
                                                                                     in1=acc[:, 0:512], op0=ALU.mult, op1=ALU.add),
                       [ubr, accr, 'vecs'], [accr])
                    OP('dve', lambda e, acc=acc, ub=ub, ch=ch: e.scalar_tensor_tensor(out=acc[:, 0:512], in0=ub[:, 0:512],
                                                                                     scalar=vecs[:, l, 62 + ch * 3:62 + ch * 3 + 1],
                                                                                     in1=acc[:, 0:512], op0=ALU.mult, op1=ALU.add),
                       [ubr, accr, 'vecs'], [accr])
                    OP('pool', lambda e, ub=ub, ch=ch: e.tensor_copy(out=tails[:, l, ch, :], in_=ub[:, 512:514]), [ubr, TL], [TL])
                    accs.append((acc, accr))
                OP('act', lambda e: e.activation(out=accs[0][0][:, 0:512], in_=accs[0][0][:, 0:512], func=AF.Silu), [accs[0][1]], [accs[0][1]])
                OP('dve', lambda e, j=j: e.tensor_tensor(out=big[:, j, :], in0=accs[0][0][:, 0:512], in1=accs[1][0][:, 0:512], op=ALU.mult),
                   [accs[0][1], accs[1][1]], ['big%d' % j])
            if s == 0:
                OP('dve', lambda e: e.tensor_scalar(out=tails[:, l], in0=tails[:, l], scalar1=hv[:, 0:1], scalar2=None, op0=ALU.mult),
                   [TL, 'hv'], [TL])
            for c in range(8):
                w, wr = st_wdn.cur()
                b, br = bank()
                mm_group(ps[:, b, :], [(w[:, j, :], big[:, j, :]) for j in range(NJ)], [wr] + ['big%d' % j for j in range(NJ)], [br])
                st_wdn.done()
                OP('dve', lambda e, c=c, b=b: e.tensor_tensor(out=X[:, xb, c, :], in0=ps[:, b, :], in1=X[:, xb, c, :], op=ALU.add),
                   [br, 'X%d_%d' % (xb, c)], ['X%d_%d' % (xb, c)])
                if store:
                    so = s if store_all else s - 1
                    S.dma('sp', 'o%d_%d' % (xb, c), out_d[c, :, so * 512:(so + 1) * 512], X[:, xb, c, :], reads=['X%d_%d' % (xb, c)])

        load_x(0)
        for s in range(nsub):
            if s + 1 < nsub:
                load_x(s + 1)
            for l in range(2):
                mixer(s, l)
                ffn(s, l, store=(l == 1 and (s >= 1 or store_all)))
        S.wait_all('sp')
        print("sched: ops=%d waits=%d" % (S.nops, S.nwaits))
        build_program.oplog = S.oplog
    return nc


def _consts():
    i = np.arange(128)
    maskc = (i[:, None] <= i[None, :]).astype(np.float32)
    maskp = (i[:, None] > i[None, :]).astype(np.float32)
    windows = (2, 4, 8, 16)

    def poolmats(first):
        A = np.zeros((128, 8, 128), np.float32)
        for g, w in enumerate(windows):
            for t in range(128):
                cnt = min(t + 1, w) if first else w
                for k in range(w):
                    sidx = t - k
                    if sidx >= 0:
                        A[sidx, g * 2, t] += 1.0 / cnt
                    elif not first:
                        A[128 + sidx, g * 2 + 1, t] += 1.0 / cnt
                A[t, g * 2, t] -= 1.0
        return A
    invf = (10000.0 ** (-np.arange(0, 64, 2, dtype=np.float32) / 64)).astype(np.float32)
    return maskc, maskp, poolmats(False), poolmats(True), np.broadcast_to(invf, (128, 32)).copy()


def _prep_weights(norm1, w_in, q_norm, k_norm, sinks, w_pool, pool_scale, sgu_v_norm, w_s, b_s,
                  w_proj_a, w_proj_b, w_proj_c, w_out, norm2, w_up, conv_w, conv_b, w_down):
    f = np.float32
    L = 2
    qcols = np.empty(512, np.int64)
    for c in range(4):
        for half in range(2):
            h = c + 4 * half
            qcols[c * 128 + half * 64:(c * 128 + half * 64 + 64)] = 256 + h * 64 + np.arange(64)
    tmcols = np.concatenate([qcols, np.arange(768, 896), np.arange(896, 1024), np.arange(1280, 1536), np.arange(0, 256)])
    wtm = np.ascontiguousarray(w_in[:, :, tmcols].reshape(L, 8, 128, 1280).transpose(0, 2, 1, 3), dtype=f)
    wu = np.ascontiguousarray(w_in[:, :, 1024:1280].reshape(L, 8, 128, 2, 128).transpose(0, 2, 3, 1, 4), dtype=f)
    gates = w_in[:, :, 1536:].reshape(L, 8, 128, 3, 8, 128)
    gates = gates.transpose(0, 4, 2, 3, 1, 5).reshape(L, 8, 128, 24, 128)
    proj = np.concatenate([w_proj_a, w_proj_b, w_proj_c], axis=1)
    proj = proj.reshape(L, 8, 128, 8, 128).transpose(0, 3, 2, 1, 4)
    wgp = np.ascontiguousarray(np.concatenate([gates, proj], axis=3), dtype=f)
    wout = np.ascontiguousarray(w_out.reshape(L, 8, 128, 8, 128).transpose(0, 3, 2, 1, 4), dtype=f)
    wup = w_up.reshape(L, 8, 128, 2, NJ, 128).transpose(0, 4, 2, 3, 1, 5)
    wup = np.ascontiguousarray(wup, dtype=f)
    wdn = np.ascontiguousarray(w_down.reshape(L, NJ, 128, 8, 128).transpose(0, 3, 2, 1, 4), dtype=f)
    vecs = np.zeros((L, 128, NV), f)
    vecs[:, :, 0:8] = norm1.reshape(L, 8, 128).transpose(0, 2, 1)
    vecs[:, :, 8:16] = norm2.reshape(L, 8, 128).transpose(0, 2, 1)
    vecs[:, :, 16:18] = pool_scale.reshape(L, 2, 128).transpose(0, 2, 1)
    vecs[:, :, 18:62] = conv_b.reshape(L, 44, 128).transpose(0, 2, 1)
    vecs[:, :, 62:194] = conv_w.reshape(L, 3, 44, 128).transpose(0, 3, 2, 1).reshape(L, 128, 132)
    rows = np.zeros((L, 128, NR), f)
    rows[:, :, 0:64] = q_norm[:, None, :]
    rows[:, :, 64:128] = k_norm[:, None, :]
    rows[:, :, 128:192] = sgu_v_norm[:, None, :]
    rows[:, :, 192:200] = sinks[:, None, :]
    brep = np.ascontiguousarray(b_s.reshape(L, 2, 2, 1, 128).repeat(64, axis=3).transpose(0, 2, 3, 1, 4).reshape(L, 128, 2, 128), dtype=f)
    wsT = np.ascontiguousarray(w_s.transpose(0, 3, 1, 2), dtype=f)
    wpbd = np.zeros((L, 128, 2, 128), f)
    for cc in range(2):
        for h in range(2):
            wpbd[:, h * 64:(h + 1) * 64, cc, h * 64:(h + 1) * 64] = w_pool[:, cc * 2 + h]
    return dict(wtm=wtm, wu=wu, wgp=wgp, wout=wout, wup=wup, wdn=wdn, vecs=vecs, rows=rows, brep=brep, wsT=wsT, wpbd=wpbd)


_NC_CACHE = {}


def kernel(x, positions, norm1, w_in, q_norm, k_norm, sinks, w_pool, pool_scale, sgu_v_norm, w_s, b_s,
           w_proj_a, w_proj_b, w_proj_c, w_out, norm2, w_up, conv_w, conv_b, w_down):
    args = [np.asarray(a) for a in (norm1, w_in, q_norm, k_norm, sinks, w_pool, pool_scale, sgu_v_norm, w_s, b_s,
                                    w_proj_a, w_proj_b, w_proj_c, w_out, norm2, w_up, conv_w, conv_b, w_down)]
    x = np.asarray(x, dtype=np.float32)
    positions = np.asarray(positions).astype(np.int32)
    W = _prep_weights(*args)
    maskc, maskp, poolA, poolA_first, invf = _consts()
    in_maps = []
    for core in range(NCORES):
        b, j = core // 4, core % 4
        t0 = j * OWN - HALO
        xs = np.zeros((WIN, D), np.float32)
        ps_ = np.zeros((WIN,), np.int32)
        lo = max(t0, 0)
        xs[lo - t0:] = x[b, lo:t0 + WIN]
        ps_[lo - t0:] = positions[b, lo:t0 + WIN]
        first = (j == 0)
        m = dict(W)
        m["xw"] = np.ascontiguousarray(xs.T.reshape(8, 128, WIN))
        m["posw"] = np.ascontiguousarray(ps_.reshape(NBLK, 128).T)
        m["hv"] = np.full((128, 1), 0.0 if first else 1.0, np.float32)
        m["mask4"] = np.zeros_like(maskp) if first else maskp
        m["poolA4"] = poolA_first if first else poolA
        m["maskc"] = maskc
        m["maskp"] = maskp
        m["poolA"] = poolA
        m["invf"] = invf
        in_maps.append(m)
    if "nc" not in _NC_CACHE:
        _NC_CACHE["nc"] = build_program()
    res = run_bass_kernel_spmd(_NC_CACHE["nc"], in_maps, core_ids=list(range(NCORES)))
    y = np.empty((2, SEQ, D), np.float32)
    for core in range(NCORES):
        b, j = core // 4, core % 4
        o = res.results[core]["out"]
        y[b, j * OWN:(j + 1) * OWN, :] = o.reshape(D, OWN).T
    return y
```

```python
import math
import numpy as np
from contextlib import ExitStack
import concourse.bass as bass
import concourse.mybir as mybir
from concourse.alu_op_type import AluOpType as ALU
from concourse.bass_utils import run_bass_kernel_spmd

AF = mybir.ActivationFunctionType
F32 = mybir.dt.float32
BF16 = mybir.dt.bfloat16
I32 = mybir.dt.int32
AX = mybir.AxisListType

ENG = ['pe', 'act', 'dve', 'pool', 'sp']
NCORES = 8
D = 1024
SEQ = 8192
OWN = 2048
HALO = 512
WIN = OWN + HALO
NSUB = WIN // 512
NBLK = WIN // 128
DFF = 2816
NJ = DFF // 128
EPS = 1e-6
NV = 8 + 8 + 2 + 44 + 132
NR = 64 * 3 + 8


class Sched:
    def __init__(self, nc, stack, self_sync=('act', 'dve', 'pool')):
        self.nc = nc
        self.stack = stack
        self.engs = {'pe': nc.tensor, 'act': nc.scalar, 'dve': nc.vector,
                     'pool': nc.gpsimd, 'sp': nc.sync}
        self.sem = {e: stack.enter_context(nc.semaphore("s_" + e)) for e in ENG}
        self.cnt = {e: 0 for e in ENG}
        self.known = {e: {} for e in ENG}
        self.snap = {}
        self.lastw = {}
        self.readers = {}
        self.dsem = {}
        self.dcnt = {}
        self.self_sync = set(self_sync)
        self.nwaits = 0
        self.nops = 0
        self.oplog = []

    def _handle(self, key):
        return self.sem[key] if key in self.sem else self.dsem[key]

    def _merge(self, e, key, val):
        k = self.known[e]
        if k.get(key, 0) < val:
            k[key] = val
        s = self.snap.get((key, val))
        if s:
            for kk, vv in s.items():
                if k.get(kk, 0) < vv:
                    k[kk] = vv

    def _deps(self, e, reads, writes):
        need = {}

        def add(kv):
            if kv is None:
                return
            key, val = kv
            if key == e and e not in self.self_sync:
                return
            if need.get(key, 0) < val:
                need[key] = val
        for r in reads:
            add(self.lastw.get(r))
            if r.startswith('ps'):
                for kv in self.readers.get(r, {}).items():
                    if kv[0] != e:
                        add(kv)
        for r in writes:
            add(self.lastw.get(r))
            for kv in self.readers.get(r, {}).items():
                add(kv)
        eng = self.engs[e]
        for key, val in need.items():
            if self.known[e].get(key, 0) >= val:
                continue
            eng.wait_ge(self._handle(key), val)
            self.nwaits += 1
            self._merge(e, key, val)

    def _record(self, key, val, reads, writes):
        for r in reads:
            self.readers.setdefault(r, {})[key] = val
        for r in writes:
            self.lastw[r] = (key, val)
            self.readers[r] = {}

    limit = None

    def op(self, e, emit, reads=(), writes=()):
        if self.limit is not None and self.nops >= self.limit:
            return None
        self._deps(e, reads, writes)
        import sys as _sys
        self.oplog.append((self.nops, e, _sys._getframe(2).f_lineno, list(writes)))
        inst = emit(self.engs[e])
        self.cnt[e] += 1
        val = self.cnt[e]
        inst.then_inc(self.sem[e], 1)
        if e not in self.self_sync:
            self.known[e][e] = val
        self.snap[(e, val)] = dict(self.known[e])
        self._record(e, val, reads, writes)
        self.nops += 1
        return inst

    def dma(self, q, semname, out, in_, reads=(), writes=()):
        if self.limit is not None and self.nops >= self.limit:
            return None
        if semname not in self.dsem:
            self.dsem[semname] = self.stack.enter_context(self.nc.semaphore("d_" + semname))
            self.dcnt[semname] = 0
        self._deps(q, reads, writes)
        inst = self.engs[q].dma_start(out=out, in_=in_)
        self.dcnt[semname] += 16
        val = self.dcnt[semname]
        inst.then_inc(self.dsem[semname], 16)
        self.snap[(semname, val)] = dict(self.known[q])
        self._record(semname, val, reads, writes)
        return inst

    def wait_all(self, e):
        eng = self.engs[e]
        for k in ENG:
            if k != e and self.cnt[k] > 0:
                eng.wait_ge(self.sem[k], self.cnt[k])
        for k, v in self.dcnt.items():
            eng.wait_ge(self.dsem[k], v)


class Stream:
    def __init__(self, S, name, slots, seq):
        self.S = S
        self.name = name
        self.slots = slots
        self.seq = seq
        self.nload = 0
        self.nuse = 0
        for _ in range(len(slots)):
            self.load_next()

    def res(self, k):
        return "%s_%d" % (self.name, k)

    def load_next(self):
        if self.nload >= len(self.seq):
            return
        k = self.nload % len(self.slots)
        self.S.dma('pool', self.res(k), self.slots[k], self.seq[self.nload], writes=[self.res(k)])
        self.nload += 1

    def cur(self):
        k = self.nuse % len(self.slots)
        return self.slots[k], self.res(k)

    def done(self):
        self.nuse += 1
        self.load_next()


def build_program(nsub=NSUB, store_all=False, limit=None):
    nc = bass.Bass("TRN2", target_bir_lowering=False)

    def din(name, shape, dt=F32):
        return nc.dram_tensor(name, list(shape), dt, kind="ExternalInput").ap()

    xw = din("xw", [8, 128, WIN])
    posw = din("posw", [128, NBLK], I32)
    hv_d = din("hv", [128, 1])
    mask4_d = din("mask4", [128, 128])
    poolA4_d = din("poolA4", [128, 8, 128])
    maskc_d = din("maskc", [128, 128])
    maskp_d = din("maskp", [128, 128])
    poolA_d = din("poolA", [128, 8, 128])
    invf_d = din("invf", [128, 32])
    vecs_d = din("vecs", [2, 128, NV])
    rows_d = din("rows", [2, 128, NR])
    brep_d = din("brep", [2, 128, 2, 128])
    wsT_d = din("wsT", [2, 128, 4, 128])
    wpbd_d = din("wpbd", [2, 128, 2, 128])
    wtm_d = din("wtm", [2, 128, 8, 1280])
    wu_d = din("wu", [2, 128, 2, 8, 128])
    wgp_d = din("wgp", [2, 8, 128, 32, 128])
    wout_d = din("wout", [2, 8, 128, 8, 128])
    wup_d = din("wup", [2, NJ, 128, 2, 8, 128])
    wdn_d = din("wdn", [2, 8, 128, NJ, 128])
    out_d = nc.dram_tensor("out", [8, 128, OWN], F32, kind="ExternalOutput").ap()

    with ExitStack() as st:
        S = Sched(nc, st)
        S.limit = limit

        def sb(name, shape, dt):
            return st.enter_context(nc.sbuf_tensor(name, list(shape), dt))

        X = sb("X", [128, 2, 8, 512], F32)
        big = sb("big", [128, 22, 512], BF16)
        hT = sb("hT", [128, 8, 512], BF16)
        rs = sb("rs", [128, 512], F32)
        NF = 10
        Fr = sb("Fr", [128, NF, 640], F32)
        kT = sb("kT", [128, 2, 640], BF16)
        qT = sb("qT", [128, 4, 512], BF16)
        Vaug = sb("Vaug", [128, 2, 5, 2, 66], BF16)
        xab = sb("xab", [128, 2, 5, 256], BF16)
        vgb = sb("vgb", [128, 2, 256], BF16)
        qkn = sb("qkn", [128, 2, 640], BF16)
        ex = sb("ex", [128, 4, 512], BF16)
        PT = sb("PT", [128, 4, 512], BF16)
        attn_tm = sb("attn_tm", [128, 2, 512], BF16)
        sm = sb("sm", [128, 2, 48], F32)
        uT = sb("uT", [128, 2, 512], F32)
        diffT = sb("diffT", [128, 2, 512], BF16)
        aT = sb("aT", [128, 2, 512], BF16)
        tails = sb("tails", [128, 2, 44, 2], F32)
        ident = sb("ident", [128, 128], BF16)
        ones = sb("ones", [128, 128], BF16)
        maskc = sb("maskc_s", [128, 128], BF16)
        maskp = sb("maskp_s", [128, 128], BF16)
        mask4 = sb("mask4_s", [128, 128], BF16)
        poolA = sb("poolA_s", [128, 8, 128], BF16)
        poolA4 = sb("poolA4_s", [128, 8, 128], BF16)
        wsT = sb("wsT_s", [128, 2, 4, 128], BF16)
        wpbd = sb("wpbd_s", [128, 2, 2, 128], BF16)
        brep = sb("brep_s", [128, 2, 2, 128], F32)
        vecs = sb("vecs_s", [128, 2, NV], F32)
        rows = sb("rows_s", [128, 2, NR], F32)
        esink = sb("esink", [128, 2, 8], F32)
        hv = sb("hv_s", [128, 1], F32)
        cst = sb("cst", [128, 4], F32)
        invf = sb("invf_s", [128, 32], F32)
        posi = sb("posi", [128, NBLK], I32)
        posf = sb("posf", [128, NBLK], F32)
        cosT = sb("cosT", [128, NBLK, 32], F32)
        sinT = sb("sinT", [128, NBLK, 32], F32)
        Wtm = sb("Wtm", [128, 8, 1280], BF16)
        Wu = sb("Wu", [128, 2, 8, 128], BF16)
        Wgp = sb("Wgp", [128, 2, 32, 128], BF16)
        Wout = sb("Wout", [128, 2, 8, 128], BF16)
        Wup = sb("Wup", [128, 2, 2, 8, 128], BF16)
        Wdn = sb("Wdn", [128, 2, NJ, 128], BF16)
        ps = st.enter_context(nc.psum_tensor("ps", [128, 8, 512], F32))

        state = {'bank': 0, 'f': 0}

        def bank():
            b = state['bank']
            state['bank'] = (b + 1) % 8
            return b, "ps%d" % b

        def fslot():
            f = state['f']
            state['f'] = (f + 1) % NF
            return Fr[:, f, :], "F%d" % f

        def mm_group(out_ap, pairs, reads, writes):
            n = len(pairs)

            def emit(e):
                inst = None
                for i, (l, r) in enumerate(pairs):
                    inst = e.matmul(out_ap, lhsT=l, rhs=r, start=(i == 0), stop=(i == n - 1))
                return inst
            S.op('pe', emit, reads=reads, writes=writes)

        def OP(e, fn, reads, writes):
            S.op(e, fn, reads=reads, writes=writes)

        order = [(s, l) for s in range(nsub) for l in range(2)]
        st_wtm = Stream(S, "Wtm", [Wtm[:]], [wtm_d[l] for (s, l) in order])
        st_wu = Stream(S, "Wu", [Wu[:]], [wu_d[l] for (s, l) in order])
        st_wgp = Stream(S, "Wgp", [Wgp[:, k] for k in range(2)], [wgp_d[l, c] for (s, l) in order for c in range(8)])
        st_wout = Stream(S, "Wout", [Wout[:, k] for k in range(2)], [wout_d[l, c] for (s, l) in order for c in range(8)])
        st_wup = Stream(S, "Wup", [Wup[:, k] for k in range(2)], [wup_d[l, j] for (s, l) in order for j in range(NJ)])
        st_wdn = Stream(S, "Wdn", [Wdn[:, k] for k in range(2)], [wdn_d[l, c] for (s, l) in order for c in range(8)])

        S.dma('pool', 'c_maskc', maskc[:], maskc_d, writes=['maskc'])
        S.dma('pool', 'c_maskp', maskp[:], maskp_d, writes=['maskp'])
        S.dma('pool', 'c_mask4', mask4[:], mask4_d, writes=['mask4'])
        S.dma('pool', 'c_poolA', poolA[:], poolA_d, writes=['poolA'])
        S.dma('pool', 'c_poolA4', poolA4[:], poolA4_d, writes=['poolA4'])
        for l in range(2):
            S.dma('pool', 'c_wsT%d' % l, wsT[:, l], wsT_d[l], writes=['wsT%d' % l])
            S.dma('pool', 'c_wpbd%d' % l, wpbd[:, l], wpbd_d[l], writes=['wpbd%d' % l])
            S.dma('sp', 'c_brep%d' % l, brep[:, l], brep_d[l], writes=['brep%d' % l])
            S.dma('sp', 'c_vecs%d' % l, vecs[:, l], vecs_d[l], writes=['vecs'])
            S.dma('sp', 'c_rows%d' % l, rows[:, l], rows_d[l], writes=['rows'])
        S.dma('sp', 'c_hv', hv[:], hv_d, writes=['hv'])
        S.dma('sp', 'c_invf', invf[:], invf_d, writes=['invf'])
        S.dma('sp', 'c_pos', posi[:], posw, writes=['posi'])

        OP('pool', lambda e: e.memset(ident[:], 0.0), [], ['ident'])
        OP('pool', lambda e: e.affine_select(out=ident[:], in_=ident[:], pattern=[[-1, 128]],
                                             compare_op=ALU.not_equal, fill=1.0, base=0,
                                             channel_multiplier=1), ['ident'], ['ident'])
        OP('dve', lambda e: e.memset(ones[:], 1.0), [], ['ones'])
        OP('dve', lambda e: e.memset(cst[:, 0:1], -0.5), [], ['cst'])
        OP('dve', lambda e: e.memset(cst[:, 1:2], EPS), ['cst'], ['cst'])
        OP('dve', lambda e: e.memset(tails[:], 0.0), [], ['tails0', 'tails1'])
        OP('dve', lambda e: e.memset(kT[:], 0.0), [], ['kT0', 'kT1'])
        OP('dve', lambda e: e.memset(Vaug[:], 1.0), [], ['Vaug0', 'Vaug1'])
        OP('dve', lambda e: e.memset(xab[:], 0.0), [], ['xab0', 'xab1'])
        for l in range(2):
            OP('dve', lambda e, l=l: e.tensor_tensor(out=wsT[:, l], in0=wsT[:, l],
                                                     in1=maskc[:].unsqueeze(1).broadcast_to([128, 4, 128]), op=ALU.mult),
               ['wsT%d' % l, 'maskc'], ['wsT%d' % l])
            OP('act', lambda e, l=l: e.activation(out=esink[:, l, :], in_=rows[:, l, 192:200], func=AF.Exp),
               ['rows'], ['esink'])

        OP('dve', lambda e: e.tensor_copy(out=posf[:], in_=posi[:]), ['posi'], ['posf'])
        ang, angr = fslot()
        ang = ang[:, 0:NBLK * 32].rearrange("p (a b) -> p a b", a=NBLK)
        OP('dve', lambda e: e.tensor_tensor(out=ang, in0=posf[:].unsqueeze(2).broadcast_to([128, NBLK, 32]),
                                            in1=invf[:].unsqueeze(1).broadcast_to([128, NBLK, 32]), op=ALU.mult),
           ['posf', 'invf'], [angr])
        MAGIC = 12582912.0
        TWO_PI = 2.0 * math.pi
        C1 = 6.28125
        C2 = float(np.float32(TWO_PI - C1))
        C3 = float(TWO_PI - C1 - C2)
        tk, tkr = fslot()
        tk = tk[:, 0:NBLK * 32].rearrange("p (a b) -> p a b", a=NBLK)
        rr, rrr = fslot()
        rr = rr[:, 0:NBLK * 32].rearrange("p (a b) -> p a b", a=NBLK)
        r2, r2r = fslot()
        r2 = r2[:, 0:NBLK * 32].rearrange("p (a b) -> p a b", a=NBLK)
        OP('dve', lambda e: e.tensor_scalar(out=tk, in0=ang, scalar1=1.0 / TWO_PI, scalar2=MAGIC, op0=ALU.mult, op1=ALU.add), [angr], [tkr])
        OP('dve', lambda e: e.tensor_scalar(out=tk, in0=tk, scalar1=-MAGIC, scalar2=None, op0=ALU.add), [tkr], [tkr])
        OP('dve', lambda e: e.scalar_tensor_tensor(out=rr, in0=tk, scalar=-C1, in1=ang, op0=ALU.mult, op1=ALU.add), [tkr, angr], [rrr])
        OP('dve', lambda e: e.scalar_tensor_tensor(out=rr, in0=tk, scalar=-C2, in1=rr, op0=ALU.mult, op1=ALU.add), [tkr, rrr], [rrr])
        OP('dve', lambda e: e.scalar_tensor_tensor(out=rr, in0=tk, scalar=-C3, in1=rr, op0=ALU.mult, op1=ALU.add), [tkr, rrr], [rrr])
        OP('dve', lambda e: e.tensor_scalar(out=r2, in0=rr, scalar1=math.pi / 2, scalar2=None, op0=ALU.add), [rrr], [r2r])
        OP('dve', lambda e: e.tensor_scalar(out=tk, in0=r2, scalar1=math.pi, scalar2=-TWO_PI, op0=ALU.is_gt, op1=ALU.mult), [r2r], [tkr])
        OP('dve', lambda e: e.tensor_tensor(out=r2, in0=r2, in1=tk, op=ALU.add), [r2r, tkr], [r2r])
        PI_SAFE = 3.1415925
        OP('dve', lambda e: e.tensor_scalar(out=rr, in0=rr, scalar1=PI_SAFE, scalar2=-PI_SAFE, op0=ALU.min, op1=ALU.max), [rrr], [rrr])
        OP('dve', lambda e: e.tensor_scalar(out=r2, in0=r2, scalar1=PI_SAFE, scalar2=-PI_SAFE, op0=ALU.min, op1=ALU.max), [r2r], [r2r])
        OP('act', lambda e: e.activation(out=sinT[:], in_=rr, func=AF.Sin), [rrr], ['sinT'])
        OP('act', lambda e: e.activation(out=cosT[:], in_=r2, func=AF.Sin), [r2r], ['cosT'])

        def load_x(s):
            xb = s % 2
            for c in range(8):
                S.dma('sp', 'x%d_%d' % (xb, c), X[:, xb, c, :], xw[c, :, s * 512:(s + 1) * 512],
                      writes=['X%d_%d' % (xb, c)])

        def rmsnorm(xb, l, gcol):
            for c in range(8):
                OP('act', lambda e, c=c: e.activation(out=big[:, 8 + c, :], in_=X[:, xb, c, :], func=AF.Square),
                   ['X%d_%d' % (xb, c)], ['big%d' % (8 + c)])
            b, br = bank()
            mm_group(ps[:, b, :], [(ones[:], big[:, 8 + c, :]) for c in range(8)],
                     ['ones'] + ['big%d' % (8 + c) for c in range(8)], [br])
            OP('act', lambda e: e.activation(out=rs[:], in_=ps[:, b, :], func=AF.Sqrt, scale=1.0 / D, bias=cst[:, 1:2]),
               [br, 'cst'], ['rs'])
            OP('dve', lambda e: e.reciprocal(out=rs[:], in_=rs[:]), ['rs'], ['rs'])
            for c in range(8):
                OP('dve', lambda e, c=c: e.scalar_tensor_tensor(out=hT[:, c, :], in0=X[:, xb, c, :], scalar=vecs[:, l, gcol + c:gcol + c + 1],
                                                               in1=rs[:], op0=ALU.mult, op1=ALU.mult),
                   ['X%d_%d' % (xb, c), 'rs', 'vecs'], ['hT%d' % c])

        HT = ['hT%d' % c for c in range(8)]

        def gelu2(src_ap, src_res, n, dst_ap, dst_res):
            t1, t1r = fslot()
            t1 = t1[:, 0:n]
            OP('act', lambda e: e.activation(out=t1, in_=src_ap, func=AF.Square), [src_res], [t1r])
            OP('dve', lambda e: e.tensor_scalar(out=t1, in0=t1, scalar1=0.044715, scalar2=1.0, op0=ALU.mult, op1=ALU.add), [t1r], [t1r])
            OP('dve', lambda e: e.tensor_tensor(out=t1, in0=t1, in1=src_ap, op=ALU.mult), [t1r, src_res], [t1r])
            OP('act', lambda e: e.activation(out=t1, in_=t1, func=AF.Tanh, scale=0.7978845608028654), [t1r], [t1r])
            OP('dve', lambda e: e.scalar_tensor_tensor(out=dst_ap, in0=t1, scalar=1.0, in1=src_ap, op0=ALU.add, op1=ALU.mult),
               [t1r, src_res], [dst_res])

        def mixer(s, l):
            xb = s % 2
            rmsnorm(xb, l, 0)
            w, wr = st_wu.cur()
            for cc in range(2):
                b, br = bank()
                mm_group(ps[:, b, :], [(w[:, cc, kc, :], hT[:, kc, :]) for kc in range(8)], [wr] + HT, [br])
                gelu2(ps[:, b, :], br, 512, uT[:, cc, :], 'uT%d' % cc)
            st_wu.done()
            wtm, wtmr = st_wtm.cur()
            KT = 'kT%d' % l
            OP('dve', lambda e: e.tensor_copy(out=kT[:, l, 0:128], in_=kT[:, l, 512:640]), [KT], [KT])
            OP('dve', lambda e: e.tensor_copy(out=Vaug[:, l, 0], in_=Vaug[:, l, 4]), ['Vaug%d' % l], ['Vaug%d' % l])
            OP('dve', lambda e: e.tensor_copy(out=xab[:, l, 0], in_=xab[:, l, 4]), ['xab%d' % l], ['xab%d' % l])
            for bi in range(4):
                blk = s * 4 + bi
                tb = bi * 128
                par = bi % 2
                bA, bAr = bank()
                bB, bBr = bank()
                bC, bCr = bank()
                mm_group(ps[:, bA, :], [(hT[:, kc, tb:tb + 128], wtm[:, kc, 0:512]) for kc in range(8)], [wtmr] + HT, [bAr])
                mm_group(ps[:, bB, :], [(hT[:, kc, tb:tb + 128], wtm[:, kc, 512:1024]) for kc in range(8)], [wtmr] + HT, [bBr])
                mm_group(ps[:, bC, 0:256], [(hT[:, kc, tb:tb + 128], wtm[:, kc, 1024:1280]) for kc in range(8)], [wtmr] + HT, [bCr])
                if bi == 3:
                    st_wtm.done()
                SM = 'sm%d' % par
                ss = sm[:, par, 0:14]
                rq = sm[:, par, 16:30]
                t1, t1r = fslot()
                OP('act', lambda e: e.activation(out=t1[:, 0:512], in_=ps[:, bA, :], func=AF.Square), [bAr], [t1r])
                OP('act', lambda e: e.activation(out=t1[:, 512:640], in_=ps[:, bB, 0:128], func=AF.Square), [bBr, t1r], [t1r])
                OP('dve', lambda e: e.tensor_reduce(out=ss[:, 0:10], in_=t1[:, 0:640].rearrange("p (a b) -> p a b", a=10), axis=AX.X, op=ALU.add),
                   [t1r], [SM])
                gl, glr = fslot()
                gelu2(ps[:, bB, 256:512], bBr, 256, gl[:, 0:256], glr)
                OP('act', lambda e: e.activation(out=gl[:, 256:512], in_=gl[:, 0:256], func=AF.Square, scale=0.5), [glr], [glr])
                OP('dve', lambda e: e.tensor_reduce(out=ss[:, 10:14], in_=gl[:, 256:512].rearrange("p (a b) -> p a b", a=4), axis=AX.X, op=ALU.add),
                   [glr, SM], [SM])
                OP('dve', lambda e: e.tensor_scalar(out=rq, in0=ss, scalar1=1.0 / 64, scalar2=EPS, op0=ALU.mult, op1=ALU.add), [SM], [SM])
                OP('pool', lambda e: e.tensor_tensor(out=rq, in0=rq, in1=cst[:, 0:1].broadcast_to([128, 14]), op=ALU.pow), [SM, 'cst'], [SM])
                qkg, qkgr = fslot()
                OP('dve', lambda e: e.tensor_tensor(out=qkg[:, 0:512].rearrange("p (a b) -> p a b", a=8),
                                                    in0=ps[:, bA, :].rearrange("p (a b) -> p a b", a=8),
                                                    in1=rows[:, l, 0:64].unsqueeze(1).broadcast_to([128, 8, 64]), op=ALU.mult),
                   [bAr, 'rows'], [qkgr])
                OP('dve', lambda e: e.tensor_tensor(out=qkg[:, 512:640].rearrange("p (a b) -> p a b", a=2),
                                                    in0=ps[:, bB, 0:128].rearrange("p (a b) -> p a b", a=2),
                                                    in1=rows[:, l, 64:128].unsqueeze(1).broadcast_to([128, 2, 64]), op=ALU.mult),
                   [bBr, 'rows', qkgr], [qkgr])
                H = qkg[:, 0:640].rearrange("p (a h b) -> p a h b", a=10, h=2)
                t1v, t2v = H[:, :, 0, :], H[:, :, 1, :]
                cb_ = cosT[:, blk, :].unsqueeze(1).broadcast_to([128, 10, 32])
                sb_ = sinT[:, blk, :].unsqueeze(1).broadcast_to([128, 10, 32])
                ta, tar = fslot()
                A_ = ta[:, 0:320].rearrange("p (a b) -> p a b", a=10)
                B_ = ta[:, 320:640].rearrange("p (a b) -> p a b", a=10)
                tc_, tcr = fslot()
                C_ = tc_[:, 0:320].rearrange("p (a b) -> p a b", a=10)
                D_ = tc_[:, 320:640].rearrange("p (a b) -> p a b", a=10)
                ro, ror = fslot()
                RO = ro[:, 0:640].rearrange("p (a h b) -> p a h b", a=10, h=2)
                OP('dve', lambda e: e.tensor_tensor(out=A_, in0=t1v, in1=cb_, op=ALU.mult), [qkgr, 'cosT'], [tar])
                OP('dve', lambda e: e.tensor_tensor(out=B_, in0=t2v, in1=sb_, op=ALU.mult), [qkgr, 'sinT', tar], [tar])
                OP('dve', lambda e: e.tensor_tensor(out=C_, in0=t2v, in1=cb_, op=ALU.mult), [qkgr, 'cosT'], [tcr])
                OP('dve', lambda e: e.tensor_tensor(out=D_, in0=t1v, in1=sb_, op=ALU.mult), [qkgr, 'sinT', tcr], [tcr])
                OP('dve', lambda e: e.tensor_tensor(out=RO[:, :, 0, :], in0=A_, in1=B_, op=ALU.subtract), [tar], [ror])
                OP('dve', lambda e: e.tensor_tensor(out=RO[:, :, 1, :], in0=C_, in1=D_, op=ALU.add), [tcr, ror], [ror])
                QKN = 'qkn%d' % par
                OP('dve', lambda e: e.tensor_tensor(out=qkn[:, par, :].rearrange("p (a b) -> p a b", a=10),
                                                    in0=ro[:, 0:640].rearrange("p (a b) -> p a b", a=10),
                                                    in1=rq[:, 0:10].unsqueeze(2).broadcast_to([128, 10, 64]), op=ALU.mult),
                   [ror, SM], [QKN])
                VA = 'Vaug%d' % l
                OP('act', lambda e: e.copy(out=Vaug[:, l, bi + 1, :, 0:64], in_=ps[:, bB, 128:256].rearrange("p (a b) -> p a b", a=2)),
                   [bBr], [VA])
                OP('dve', lambda e: e.scalar_tensor_tensor(out=gl[:, 256:512].rearrange("p (a b) -> p a b", a=4),
                                                           in0=gl[:, 0:256].rearrange("p (a b) -> p a b", a=4), scalar=0.5,
                                                           in1=rows[:, l, 128:192].unsqueeze(1).broadcast_to([128, 4, 64]),
                                                           op0=ALU.mult, op1=ALU.mult), [glr, 'rows'], [glr])
                VG = 'vgb%d' % par
                OP('dve', lambda e: e.tensor_tensor(out=vgb[:, par, :].rearrange("p (a b) -> p a b", a=4),
                                                    in0=gl[:, 256:512].rearrange("p (a b) -> p a b", a=4),
                                                    in1=rq[:, 10:14].unsqueeze(2).broadcast_to([128, 4, 64]), op=ALU.mult),
                   [glr, SM], [VG])
                XA = 'xab%d' % l
                OP('act', lambda e: e.copy(out=xab[:, l, bi + 1, :], in_=ps[:, bC, 0:256]), [bCr], [XA])
                bT, bTr = bank()
                psb = ps[:, bT, :].bitcast(BF16)

                def emit_tr(e):
                    inst = None
                    for i in range(5):
                        inst = e.transpose(out=psb[:, i * 128:(i + 1) * 128], in_=qkn[:, par, i * 128:(i + 1) * 128], identity=ident[:])
                    return inst
                OP('pe', emit_tr, [QKN, 'ident'], [bTr])
                OP('act', lambda e: e.copy(out=qT[:, :, tb:tb + 128], in_=psb[:, 0:512].rearrange("p (a b) -> p a b", a=4)), [bTr], ['qT'])
                OP('act', lambda e: e.copy(out=kT[:, l, 128 + tb:256 + tb], in_=psb[:, 512:640]), [bTr], [KT])
                pvsrc = []
                for kb in range(2):
                    koff = tb + kb * 128
                    if kb == 0:
                        mk, mkr = (mask4, 'mask4') if blk == 4 else (maskp, 'maskp')
                    else:
                        mk, mkr = maskc, 'maskc'
                    for grp in range(2):
                        idx = kb * 2 + grp
                        bS, bSr = bank()
                        OP('pe', lambda e, bS=bS, grp=grp, koff=koff: e.matmul(
                            ps[:, bS, :].rearrange("p (a b) -> p a b", a=4),
                            lhsT=kT[grp * 64:(grp + 1) * 64, l, koff:koff + 128],
                            rhs=qT[grp * 64:(grp + 1) * 64, :, tb:tb + 128], start=True, stop=True),
                           [KT, 'qT'], [bSr])
                        OP('act', lambda e, bS=bS, idx=idx: e.activation(out=ex[:, idx, :], in_=ps[:, bS, :], func=AF.Exp, scale=0.125),
                           [bSr], ['ex%d' % idx])
                        OP('dve', lambda e, idx=idx, mk=mk: e.tensor_tensor(
                            out=PT[:, idx, :].rearrange("p (a b) -> p a b", a=4),
                            in0=ex[:, idx, :].rearrange("p (a b) -> p a b", a=4),
                            in1=mk[:].unsqueeze(1).broadcast_to([128, 4, 128]), op=ALU.mult),
                           ['ex%d' % idx, mkr], ['PT%d' % idx])
                obanks = []
                for grp in range(2):
                    bO, bOr = bank()
                    obanks.append((bO, bOr))

                    def emit_pv(e, grp=grp, bO=bO):
                        inst = None
                        for c in range(4):
                            for kb in range(2):
                                inst = e.matmul(ps[:, bO, c * 128:c * 128 + 65],
                                                lhsT=PT[:, kb * 2 + grp, c * 128:(c + 1) * 128],
                                                rhs=Vaug[:, l, bi + kb, grp, 0:65], start=(kb == 0), stop=(kb == 1))
                        return inst
                    OP('pe', emit_pv, ['PT%d' % grp, 'PT%d' % (2 + grp), VA], [bOr])
                den = sm[:, par, 32:40]
                for grp in range(2):
                    bO, bOr = obanks[grp]
                    OP('dve', lambda e, grp=grp, bO=bO: e.tensor_tensor(
                        out=den[:, grp * 4:(grp + 1) * 4],
                        in0=ps[:, bO, :].rearrange("p (a b) -> p a b", a=4)[:, :, 64],
                        in1=esink[:, l, grp * 4:(grp + 1) * 4], op=ALU.add), [bOr, 'esink', SM], [SM])
                OP('dve', lambda e: e.reciprocal(out=den, in_=den), [SM], [SM])
                AT = 'attn_tm%d' % par
                for grp in range(2):
                    bO, bOr = obanks[grp]
                    OP('dve', lambda e, grp=grp, bO=bO: e.tensor_tensor(
                        out=attn_tm[:, par, grp * 256:(grp + 1) * 256].rearrange("p (a b) -> p a b", a=4),
                        in0=ps[:, bO, :].rearrange("p (a b) -> p a b", a=4)[:, :, 0:64],
                        in1=den[:, grp * 4:(grp + 1) * 4].unsqueeze(2).broadcast_to([128, 4, 64]), op=ALU.mult),
                       [bOr, SM], [AT])
                bT2, bT2r = bank()
                psb2 = ps[:, bT2, :].bitcast(BF16)

                def emit_tr2(e):
                    inst = None
                    for i in range(4):
                        inst = e.transpose(out=psb2[:, i * 128:(i + 1) * 128], in_=attn_tm[:, par, i * 128:(i + 1) * 128], identity=ident[:])
                    return inst
                OP('pe', emit_tr2, [AT, 'ident'], [bT2r])
                OP('act', lambda e: e.copy(out=big[:, 16:20, tb:tb + 128], in_=psb2[:, 0:512].rearrange("p (a b) -> p a b", a=4)),
                   [bT2r], ['big16', 'big17', 'big18', 'big19'])
                bG, bGr = bank()

                def emit_sgu(e):
                    inst = None
                    for g in range(4):
                        cc = g // 2
                        inst = e.matmul(ps[:, bG, g * 128:(g + 1) * 128], lhsT=vgb[:, par, cc * 128:(cc + 1) * 128],
                                        rhs=wsT[:, l, g, :], start=True, stop=True)
                    return inst
                OP('pe', emit_sgu, [VG, 'wsT%d' % l], [bGr])
                for g in range(4):
                    cc, h = g // 2, g % 2
                    pr = slice(h * 64, (h + 1) * 64)
                    tt, ttr = fslot()
                    OP('dve', lambda e, g=g, cc=cc, pr=pr, tt=tt: e.tensor_tensor(out=tt[pr, 0:128], in0=ps[pr, bG, g * 128:(g + 1) * 128],
                                                                              in1=brep[pr, l, cc, :], op=ALU.add),
                       [bGr, 'brep%d' % l], [ttr])
                    OP('dve', lambda e, cc=cc, pr=pr, tt=tt: e.scalar_tensor_tensor(out=big[pr, 20 + cc, tb:tb + 128], in0=tt[pr, 0:128], scalar=0.5,
                                                                                   in1=uT[pr, cc, tb:tb + 128], op0=ALU.mult, op1=ALU.mult),
                       [ttr, 'uT%d' % cc], ['big%d' % (20 + cc)])
                bP, bPr = bank()
                pA = poolA4 if blk == 4 else poolA
                pAr = 'poolA4' if blk == 4 else 'poolA'

                def emit_pool(e):
                    inst = None
                    for g in range(4):
                        cc = g // 2
                        inst = e.matmul(ps[:, bP, g * 128:(g + 1) * 128], lhsT=xab[:, l, bi + 1, cc * 128:(cc + 1) * 128],
                                        rhs=pA[:, g * 2, :], start=True, stop=False)
                        inst = e.matmul(ps[:, bP, g * 128:(g + 1) * 128], lhsT=xab[:, l, bi, cc * 128:(cc + 1) * 128],
                                        rhs=pA[:, g * 2 + 1, :], start=False, stop=True)
                    return inst
                OP('pe', emit_pool, [XA, pAr], [bPr])
                for g in range(4):
                    cc, h = g // 2, g % 2
                    pr = slice(h * 64, (h + 1) * 64)
                    OP('act', lambda e, g=g, cc=cc, pr=pr: e.copy(out=diffT[pr, cc, tb:tb + 128], in_=ps[pr, bP, g * 128:(g + 1) * 128]),
                       [bPr], ['diffT%d' % cc])
            for cc in range(2):
                b, br = bank()
                mm_group(ps[:, b, :], [(wpbd[:, l, cc, :], diffT[:, cc, :])], ['wpbd%d' % l, 'diffT%d' % cc], [br])
                OP('act', lambda e, cc=cc, b=b: e.activation(out=aT[:, cc, :], in_=ps[:, b, :], func=AF.Identity, scale=vecs[:, l, 16 + cc:17 + cc]),
                   [br, 'vecs'], ['aT%d' % cc])
            srcs = [(aT, ['aT0', 'aT1'], 24, 2, None), (big, ['big16', 'big17', 'big18', 'big19'], 26, 4, 16), (big, ['big20', 'big21'], 30, 2, 20)]
            for c in range(8):
                w, wr = st_wgp.cur()
                gb = []
                for i in range(3):
                    b, br = bank()
                    mm_group(ps[:, b, :], [(w[:, i * 8 + kc, :], hT[:, kc, :]) for kc in range(8)], [wr] + HT, [br])
                    gb.append((b, br))
                yb = []
                for (buf, rres, w0, nk, off) in srcs:
                    b, br = bank()
                    if off is None:
                        pairs = [(w[:, w0 + kk, :], buf[:, kk, :]) for kk in range(nk)]
                    else:
                        pairs = [(w[:, w0 + kk, :], buf[:, off + kk, :]) for kk in range(nk)]
                    mm_group(ps[:, b, :], pairs, [wr] + rres, [br])
                    yb.append((b, br))
                st_wgp.done()
                tts = []
                for i in range(3):
                    th, thr = fslot()
                    OP('act', lambda e, i=i, th=th: e.activation(out=th[:, 0:512], in_=ps[:, gb[i][0], :], func=AF.Tanh, scale=0.5), [gb[i][1]], [thr])
                    OP('dve', lambda e, i=i, th=th: e.scalar_tensor_tensor(out=th[:, 0:512], in0=th[:, 0:512], scalar=1.0, in1=ps[:, yb[i][0], :],
                                                                           op0=ALU.add, op1=ALU.mult), [thr, yb[i][1]], [thr])
                    tts.append((th, thr))
                OP('dve', lambda e: e.tensor_tensor(out=tts[0][0][:, 0:512], in0=tts[0][0][:, 0:512], in1=tts[1][0][:, 0:512], op=ALU.add),
                   [tts[0][1], tts[1][1]], [tts[0][1]])
                OP('dve', lambda e, c=c: e.tensor_tensor(out=big[:, 8 + c, :], in0=tts[0][0][:, 0:512], in1=tts[2][0][:, 0:512], op=ALU.add),
                   [tts[0][1], tts[2][1]], ['big%d' % (8 + c)])
            for c in range(8):
                w, wr = st_wout.cur()
                b, br = bank()
                mm_group(ps[:, b, :], [(w[:, kc, :], big[:, 8 + kc, :]) for kc in range(8)], [wr] + ['big%d' % (8 + kc) for kc in range(8)], [br])
                st_wout.done()
                OP('dve', lambda e, c=c, b=b: e.scalar_tensor_tensor(out=X[:, xb, c, :], in0=ps[:, b, :], scalar=0.5, in1=X[:, xb, c, :],
                                                                     op0=ALU.mult, op1=ALU.add), [br, 'X%d_%d' % (xb, c)], ['X%d_%d' % (xb, c)])

        def ffn(s, l, store):
            xb = s % 2
            rmsnorm(xb, l, 8)
            TL = 'tails%d' % l
            for j in range(NJ):
                w, wr = st_wup.cur()
                accs = []
                pbs = []
                for gv in range(2):
                    b, br = bank()
                    mm_group(ps[:, b, :], [(w[:, gv, kc, :], hT[:, kc, :]) for kc in range(8)], [wr] + HT, [br])
                    pbs.append((b, br))
                st_wup.done()
                for gv in range(2):
                    b, br = pbs[gv]
                    ch = gv * NJ + j
                    ub, ubr = fslot()
                    acc, accr = fslot()
                    OP('pool', lambda e, ub=ub, ch=ch: e.tensor_copy(out=ub[:, 0:2], in_=tails[:, l, ch, :]), [TL], [ubr])
                    OP('act', lambda e, ub=ub, b=b: e.copy(out=ub[:, 2:514], in_=ps[:, b, :]), [br, ubr], [ubr])
                    OP('act', lambda e, acc=acc, b=b, ch=ch: e.activation(out=acc[:, 0:512], in_=ps[:, b, :], func=AF.Identity,
                                                                        scale=vecs[:, l, 62 + ch * 3 + 2:62 + ch * 3 + 3],
                                                                        bias=vecs[:, l, 18 + ch:19 + ch]), [br, 'vecs'], [accr])
                    OP('dve', lambda e, acc=acc, ub=ub, ch=ch: e.scalar_tensor_tensor(out=acc[:, 0:512], in0=ub[:, 1:513],
                                                                                     scalar=vecs[:, l, 62 + ch * 3 + 1:62 + ch * 3 + 2],
                                                                                     in1=acc[:, 0:512], op0=ALU.mult, op1=ALU.add),
                       [ubr, accr, 'vecs'], [accr])
                    OP('dve', lambda e, acc=acc, ub=ub, ch=ch: e.scalar_tensor_tensor(out=acc[:, 0:512], in0=ub[:, 0:512],
                                                                                     scalar=vecs[:, l, 62 + ch * 3:62 + ch * 3 + 1],
                                                                                     in1=acc[:, 0:512], op0=ALU.mult, op1=ALU.add),
                       [ubr, accr, 'vecs'], [accr])
                    OP('pool', lambda e, ub=ub, ch=ch: e.tensor_copy(out=tails[:, l, ch, :], in_=ub[:, 512:514]), [ubr, TL], [TL])
                    accs.append((acc, accr))
                OP('act', lambda e: e.activation(out=accs[0][0][:, 0:512], in_=accs[0][0][:, 0:512], func=AF.Silu), [accs[0][1]], [accs[0][1]])
                OP('dve', lambda e, j=j: e.tensor_tensor(out=big[:, j, :], in0=accs[0][0][:, 0:512], in1=accs[1][0][:, 0:512], op=ALU.mult),
                   [accs[0][1], accs[1][1]], ['big%d' % j])
            if s == 0:
                OP('dve', lambda e: e.tensor_scalar(out=tails[:, l], in0=tails[:, l], scalar1=hv[:, 0:1], scalar2=None, op0=ALU.mult),
                   [TL, 'hv'], [TL])
            for c in range(8):
                w, wr = st_wdn.cur()
                b, br = bank()
                mm_group(ps[:, b, :], [(w[:, j, :], big[:, j, :]) for j in range(NJ)], [wr] + ['big%d' % j for j in range(NJ)], [br])
                st_wdn.done()
                OP('dve', lambda e, c=c, b=b: e.tensor_tensor(out=X[:, xb, c, :], in0=ps[:, b, :], in1=X[:, xb, c, :], op=ALU.add),
                   [br, 'X%d_%d' % (xb, c)], ['X%d_%d' % (xb, c)])
                if store:
                    so = s if store_all else s - 1
                    S.dma('sp', 'o%d_%d' % (xb, c), out_d[c, :, so * 512:(so + 1) * 512], X[:, xb, c, :], reads=['X%d_%d' % (xb, c)])

        load_x(0)
        for s in range(nsub):
            if s + 1 < nsub:
                load_x(s + 1)
            for l in range(2):
                mixer(s, l)
                ffn(s, l, store=(l == 1 and (s >= 1 or store_all)))
        S.wait_all('sp')
        print("sched: ops=%d waits=%d" % (S.nops, S.nwaits))
        build_program.oplog = S.oplog
    return nc


def _consts():
    i = np.arange(128)
    maskc = (i[:, None] <= i[None, :]).astype(np.float32)
    maskp = (i[:, None] > i[None, :]).astype(np.float32)
    windows = (2, 4, 8, 16)

    def poolmats(first):
        A = np.zeros((128, 8, 128), np.float32)
        for g, w in enumerate(windows):
            for t in range(128):
                cnt = min(t + 1, w) if first else w
                for k in range(w):
                    sidx = t - k
                    if sidx >= 0:
                        A[sidx, g * 2, t] += 1.0 / cnt
                    elif not first:
                        A[128 + sidx, g * 2 + 1, t] += 1.0 / cnt
                A[t, g * 2, t] -= 1.0
        return A
    invf = (10000.0 ** (-np.arange(0, 64, 2, dtype=np.float32) / 64)).astype(np.float32)
    return maskc, maskp, poolmats(False), poolmats(True), np.broadcast_to(invf, (128, 32)).copy()


def _prep_weights(norm1, w_in, q_norm, k_norm, sinks, w_pool, pool_scale, sgu_v_norm, w_s, b_s,
                  w_proj_a, w_proj_b, w_proj_c, w_out, norm2, w_up, conv_w, conv_b, w_down):
    f = np.float32
    L = 2
    qcols = np.empty(512, np.int64)
    for c in range(4):
        for half in range(2):
            h = c + 4 * half
            qcols[c * 128 + half * 64:(c * 128 + half * 64 + 64)] = 256 + h * 64 + np.arange(64)
    tmcols = np.concatenate([qcols, np.arange(768, 896), np.arange(896, 1024), np.arange(1280, 1536), np.arange(0, 256)])
    wtm = np.ascontiguousarray(w_in[:, :, tmcols].reshape(L, 8, 128, 1280).transpose(0, 2, 1, 3), dtype=f)
    wu = np.ascontiguousarray(w_in[:, :, 1024:1280].reshape(L, 8, 128, 2, 128).transpose(0, 2, 3, 1, 4), dtype=f)
    gates = w_in[:, :, 1536:].reshape(L, 8, 128, 3, 8, 128)
    gates = gates.transpose(0, 4, 2, 3, 1, 5).reshape(L, 8, 128, 24, 128)
    proj = np.concatenate([w_proj_a, w_proj_b, w_proj_c], axis=1)
    proj = proj.reshape(L, 8, 128, 8, 128).transpose(0, 3, 2, 1, 4)
    wgp = np.ascontiguousarray(np.concatenate([gates, proj], axis=3), dtype=f)
    wout = np.ascontiguousarray(w_out.reshape(L, 8, 128, 8, 128).transpose(0, 3, 2, 1, 4), dtype=f)
    wup = w_up.reshape(L, 8, 128, 2, NJ, 128).transpose(0, 4, 2, 3, 1, 5)
    wup = np.ascontiguousarray(wup, dtype=f)
    wdn = np.ascontiguousarray(w_down.reshape(L, NJ, 128, 8, 128).transpose(0, 3, 2, 1, 4), dtype=f)
    vecs = np.zeros((L, 128, NV), f)
    vecs[:, :, 0:8] = norm1.reshape(L, 8, 128).transpose(0, 2, 1)
    vecs[:, :, 8:16] = norm2.reshape(L, 8, 128).transpose(0, 2, 1)
    vecs[:, :, 16:18] = pool_scale.reshape(L, 2, 128).transpose(0, 2, 1)
    vecs[:, :, 18:62] = conv_b.reshape(L, 44, 128).transpose(0, 2, 1)
    vecs[:, :, 62:194] = conv_w.reshape(L, 3, 44, 128).transpose(0, 3, 2, 1).reshape(L, 128, 132)
    rows = np.zeros((L, 128, NR), f)
    rows[:, :, 0:64] = q_norm[:, None, :]
    rows[:, :, 64:128] = k_norm[:, None, :]
    rows[:, :, 128:192] = sgu_v_norm[:, None, :]
    rows[:, :, 192:200] = sinks[:, None, :]
    brep = np.ascontiguousarray(b_s.reshape(L, 2, 2, 1, 128).repeat(64, axis=3).transpose(0, 2, 3, 1, 4).reshape(L, 128, 2, 128), dtype=f)
    wsT = np.ascontiguousarray(w_s.transpose(0, 3, 1, 2), dtype=f)
    wpbd = np.zeros((L, 128, 2, 128), f)
    for cc in range(2):
        for h in range(2):
            wpbd[:, h * 64:(h + 1) * 64, cc, h * 64:(h + 1) * 64] = w_pool[:, cc * 2 + h]
    return dict(wtm=wtm, wu=wu, wgp=wgp, wout=wout, wup=wup, wdn=wdn, vecs=vecs, rows=rows, brep=brep, wsT=wsT, wpbd=wpbd)


_NC_CACHE = {}


def kernel(x, positions, norm1, w_in, q_norm, k_norm, sinks, w_pool, pool_scale, sgu_v_norm, w_s, b_s,
           w_proj_a, w_proj_b, w_proj_c, w_out, norm2, w_up, conv_w, conv_b, w_down):
    args = [np.asarray(a) for a in (norm1, w_in, q_norm, k_norm, sinks, w_pool, pool_scale, sgu_v_norm, w_s, b_s,
                                    w_proj_a, w_proj_b, w_proj_c, w_out, norm2, w_up, conv_w, conv_b, w_down)]
    x = np.asarray(x, dtype=np.float32)
    positions = np.asarray(positions).astype(np.int32)
    W = _prep_weights(*args)
    maskc, maskp, poolA, poolA_first, invf = _consts()
    in_maps = []
    for core in range(NCORES):
        b, j = core // 4, core % 4
        t0 = j * OWN - HALO
        xs = np.zeros((WIN, D), np.float32)
        ps_ = np.zeros((WIN,), np.int32)
        lo = max(t0, 0)
        xs[lo - t0:] = x[b, lo:t0 + WIN]
        ps_[lo - t0:] = positions[b, lo:t0 + WIN]
        first = (j == 0)
        m = dict(W)
        m["xw"] = np.ascontiguousarray(xs.T.reshape(8, 128, WIN))
        m["posw"] = np.ascontiguousarray(ps_.reshape(NBLK, 128).T)
        m["hv"] = np.full((128, 1), 0.0 if first else 1.0, np.float32)
        m["mask4"] = np.zeros_like(maskp) if first else maskp
        m["poolA4"] = poolA_first if first else poolA
        m["maskc"] = maskc
        m["maskp"] = maskp
        m["poolA"] = poolA
        m["invf"] = invf
        in_maps.append(m)
    if "nc" not in _NC_CACHE:
        _NC_CACHE["nc"] = build_program()
    res = run_bass_kernel_spmd(_NC_CACHE["nc"], in_maps, core_ids=list(range(NCORES)))
    y = np.empty((2, SEQ, D), np.float32)
    for core in range(NCORES):
        b, j = core // 4, core % 4
        o = res.results[core]["out"]
        y[b, j * OWN:(j + 1) * OWN, :] = o.reshape(D, OWN).T
    return y
```
